# Optimizing a Trainium2 kernel written in Bass

```python
import jax, jax.numpy as jnp
from jax import lax
import numpy as np

D_MODEL = 1024
BATCH = 2
SEQ = 8192
DEPTH = 1

CHUNK = 64
LEFT_CHUNKS = 8
BAND_CHUNKS = LEFT_CHUNKS + 1
POOL_WIDTH = D_MODEL // 2
POOL_WINDOWS = (2, 4, 8, 16)
N_POOL_GROUPS = len(POOL_WINDOWS)
POOL_GROUP = POOL_WIDTH // N_POOL_GROUPS
N_HEADS = 8
HEAD_DIM = 64
ATTN_WIDTH = N_HEADS * HEAD_DIM
MAX_REL = 64
D_FF = 2816
N_BRANCHES = 2
IN_WIDTH = POOL_WIDTH + 3 * ATTN_WIDTH
EPS = 1e-6

kernel_name = "hybrid_pool_chunkattn_macaron"


def rmsnorm(x, g):
    xf = x.astype(jnp.float32)
    y = xf * lax.rsqrt(jnp.mean(xf * xf, axis=-1, keepdims=True) + EPS)
    return (y * g.astype(jnp.float32)).astype(x.dtype)


def swiglu(x, w_gate, w_up, w_down):
    return (jax.nn.silu(x @ w_gate) * (x @ w_up)) @ w_down


def pool_mixer(u, pool_w, pool_scale):
    b, s, _ = u.shape
    uf = u.astype(jnp.float32).reshape(b, s, N_POOL_GROUPS, POOL_GROUP)
    cz = jnp.concatenate([jnp.zeros((b, 1, N_POOL_GROUPS, POOL_GROUP), jnp.float32),
                          jnp.cumsum(uf, axis=1)], axis=1)
    t = jnp.arange(s)
    outs = []
    for g, w in enumerate(POOL_WINDOWS):
        lo = jnp.maximum(t + 1 - w, 0)
        sums = cz[:, 1:, g] - cz[:, lo, g]
        counts = jnp.minimum(t + 1, w).astype(jnp.float32)[None, :, None]
        outs.append(sums / counts - uf[:, :, g])
    mixed = jnp.stack(outs, axis=2).astype(u.dtype)
    y = jnp.einsum('bsgc,gcd->bsgd', mixed, pool_w).reshape(b, s, POOL_WIDTH)
    return y * pool_scale


def chunk_attention(q, k, v, rel_bias):
    b, s, h, dh = q.shape
    nc = s // CHUNK
    band = BAND_CHUNKS * CHUNK
    qc = q.reshape(b, nc, CHUNK, h, dh)
    pad = ((0, 0), (LEFT_CHUNKS * CHUNK, 0), (0, 0), (0, 0))
    kp = jnp.pad(k, pad).reshape(b, nc + LEFT_CHUNKS, CHUNK, h, dh)
    vp = jnp.pad(v, pad).reshape(b, nc + LEFT_CHUNKS, CHUNK, h, dh)
    idx = jnp.arange(nc)[:, None] + jnp.arange(BAND_CHUNKS)[None, :]
    kb = kp[:, idx].reshape(b, nc, band, h, dh)
    vb = vp[:, idx].reshape(b, nc, band, h, dh)
    scores = jnp.einsum('bnqhd,bnkhd->bhnqk', qc, kb).astype(jnp.float32) * (HEAD_DIM ** -0.5)
    rel = jnp.arange(CHUNK)[:, None] + LEFT_CHUNKS * CHUNK - jnp.arange(band)[None, :]
    rel_idx = jnp.clip(rel, -MAX_REL, MAX_REL) + MAX_REL
    bias = rel_bias.astype(jnp.float32)[:, rel_idx]
    scores = scores + bias[None, :, None]
    key_chunk = jnp.arange(nc)[:, None] - LEFT_CHUNKS + jnp.arange(BAND_CHUNKS)[None, :]
    valid = jnp.repeat(key_chunk >= 0, CHUNK, axis=1)
    scores = jnp.where(valid[None, None, :, None, :], scores, jnp.finfo(jnp.float32).min)
    p = jax.nn.softmax(scores, axis=-1).astype(v.dtype)
    out = jnp.einsum('bhnqk,bnkhd->bnqhd', p, vb)
    return out.reshape(b, s, h * dh)


def _normal(key, shape, scale):
    return jax.random.normal(key, shape, jnp.float32) * scale


def setup_inputs(seed: int = 0) -> dict:
    key = jax.random.key(seed)
    ks = jax.random.split(key, 24)
    L, D = DEPTH, D_MODEL
    gain = lambda k, n: 1.0 + 0.05 * jax.random.normal(k, (L, n), jnp.float32)
    return {
        "x": _normal(ks[0], (BATCH, SEQ, D), 1.0),
        "ffn1_norm": gain(ks[1], D),
        "ffn1_w_gate": _normal(ks[2], (L, D, D_FF), D ** -0.5),
        "ffn1_w_up": _normal(ks[3], (L, D, D_FF), D ** -0.5),
        "ffn1_w_down": _normal(ks[4], (L, D_FF, D), D_FF ** -0.5),
        "mix_norm": gain(ks[5], D),
        "w_in": _normal(ks[6], (L, D, IN_WIDTH), D ** -0.5),
        "pool_w": _normal(ks[7], (L, N_POOL_GROUPS, POOL_GROUP, POOL_GROUP), POOL_GROUP ** -0.5),
        "pool_scale": 1.0 + 0.1 * jax.random.normal(ks[8], (L, POOL_WIDTH), jnp.float32),
        "rel_bias": _normal(ks[9], (L, N_HEADS, 2 * MAX_REL + 1), 0.1),
        "w_branch_pool": _normal(ks[10], (L, POOL_WIDTH, D), POOL_WIDTH ** -0.5),
        "w_branch_attn": _normal(ks[11], (L, ATTN_WIDTH, D), ATTN_WIDTH ** -0.5),
        "w_gate": _normal(ks[12], (L, D, N_BRANCHES * D), D ** -0.5),
        "b_gate": _normal(ks[13], (L, N_BRANCHES * D), 0.02),
        "w_out": _normal(ks[14], (L, D, D), D ** -0.5),
        "ffn2_norm": gain(ks[15], D),
        "ffn2_w_gate": _normal(ks[16], (L, D, D_FF), D ** -0.5),
        "ffn2_w_up": _normal(ks[17], (L, D, D_FF), D ** -0.5),
        "ffn2_w_down": _normal(ks[18], (L, D_FF, D), D_FF ** -0.5),
        "final_norm": 1.0 + 0.05 * jax.random.normal(ks[19], (D,), jnp.float32),
    }


def reference(x, ffn1_norm, ffn1_w_gate, ffn1_w_up, ffn1_w_down, mix_norm, w_in,
              pool_w, pool_scale, rel_bias, w_branch_pool, w_branch_attn, w_gate, b_gate,
              w_out, ffn2_norm, ffn2_w_gate, ffn2_w_up, ffn2_w_down, final_norm):
    b, s, d = x.shape
    for l in range(DEPTH):
        x = x + 0.5 * swiglu(rmsnorm(x, ffn1_norm[l]), ffn1_w_gate[l], ffn1_w_up[l], ffn1_w_down[l])
        h = rmsnorm(x, mix_norm[l])
        u = h @ w_in[l]
        u_pool = u[..., :POOL_WIDTH]
        q = u[..., POOL_WIDTH:POOL_WIDTH + ATTN_WIDTH].reshape(b, s, N_HEADS, HEAD_DIM)
        k = u[..., POOL_WIDTH + ATTN_WIDTH:POOL_WIDTH + 2 * ATTN_WIDTH].reshape(b, s, N_HEADS, HEAD_DIM)
        v = u[..., POOL_WIDTH + 2 * ATTN_WIDTH:].reshape(b, s, N_HEADS, HEAD_DIM)
        y_pool = pool_mixer(u_pool, pool_w[l], pool_scale[l]) @ w_branch_pool[l]
        y_attn = chunk_attention(q, k, v, rel_bias[l]) @ w_branch_attn[l]
        gates = jax.nn.sigmoid(h @ w_gate[l] + b_gate[l]).reshape(b, s, N_BRANCHES, d)
        merged = gates[:, :, 0] * y_pool + gates[:, :, 1] * y_attn
        x = x + merged @ w_out[l]
        x = x + 0.5 * swiglu(rmsnorm(x, ffn2_norm[l]), ffn2_w_gate[l], ffn2_w_up[l], ffn2_w_down[l])
    return rmsnorm(x, final_norm)
```

```python
import contextlib
import numpy as np
import concourse.bass as bass
import concourse.mybir as mybir
from concourse.bass_utils import run_bass_kernel_spmd

F32 = mybir.dt.float32
BF16 = mybir.dt.bfloat16
AF = mybir.ActivationFunctionType
ALU = mybir.AluOpType

D = 1024
DFF = 2816
NJ = DFF // 128
SEQ = 8192
NCORES = 8
OWN = 2048
HALO = 512
TLOC = OWN + HALO
EPS = 1e-6
NEG = -30000.0
WINDOWS = (2, 4, 8, 16)

RING_SLOTS = 5
SLOT_ELEMS = 4096
KT_BLOCKS = 12
VROW = 768

SLOT_SIZES = {}
SLOT_ORDER = []


def _def_slots():
    for f in (1, 2):
        for jp in range(NJ // 2):
            SLOT_SIZES[f"GU{f}_{jp}"] = 4096
        for m in range(8):
            SLOT_SIZES[f"DN{f}_{m}"] = DFF
    for cq in range(3):
        SLOT_SIZES[f"IN{cq}"] = 4096
    SLOT_SIZES["INV"] = 4096
    SLOT_SIZES["PW"] = 512
    for m in range(8):
        SLOT_SIZES[f"GB{m}"] = 3072
    for mq in range(2):
        SLOT_SIZES[f"WO{mq}"] = 4096
    off = 0
    for k, v in SLOT_SIZES.items():
        SLOT_ORDER.append((k, off, v))
        off += v
    return off


WST_TOT = _def_slots()
SLOT_OFF = {k: (o, n) for k, o, n in SLOT_ORDER}

C_G1, C_G2, C_G3, C_GF, C_PS, C_BG, C_CB = 0, 8, 16, 24, 32, 36, 52
C_TOT = 64


def _colchunk(W, c0, ncol=128):
    K = W.shape[0]
    return W[:, c0:c0 + ncol].reshape(K // 128, 128, ncol).transpose(1, 0, 2).reshape(128, -1)


def _build_wst(inp):
    wst = np.empty((128, WST_TOT), np.float32)

    def put(name, arr):
        o, n = SLOT_OFF[name]
        assert arr.shape == (128, n), (name, arr.shape, n)
        wst[:, o:o + n] = arr

    for f, pre in ((1, "ffn1"), (2, "ffn2")):
        wg, wu, wd = inp[pre + "_w_gate"][0], inp[pre + "_w_up"][0], inp[pre + "_w_down"][0]
        for jp in range(NJ // 2):
            parts = []
            for jj in range(2):
                j = 2 * jp + jj
                parts += [_colchunk(wg, j * 128), _colchunk(wu, j * 128)]
            put(f"GU{f}_{jp}", np.concatenate(parts, axis=1))
        for m in range(8):
            put(f"DN{f}_{m}", _colchunk(wd, m * 128))
    w_in = inp["w_in"][0]
    for cq in range(3):
        put(f"IN{cq}", np.concatenate([_colchunk(w_in, (4 * cq + c) * 128) for c in range(4)], axis=1))
    put("INV", _colchunk(w_in, 1536, 512))
    put("PW", inp["pool_w"][0].transpose(1, 0, 2).reshape(128, 512))
    wgt, wbp, wba = inp["w_gate"][0], inp["w_branch_pool"][0], inp["w_branch_attn"][0]
    for m in range(8):
        put(f"GB{m}", np.concatenate([_colchunk(wgt, m * 128), _colchunk(wgt, (8 + m) * 128),
                                      _colchunk(wbp, m * 128), _colchunk(wba, m * 128)], axis=1))
    w_out = inp["w_out"][0]
    for mq in range(2):
        put(f"WO{mq}", np.concatenate([_colchunk(w_out, (4 * mq + c) * 128) for c in range(4)], axis=1))
    return wst


def _build_consts(inp):
    cv = np.zeros((128, C_TOT), np.float32)
    cv[:, C_G1:C_G1 + 8] = inp["ffn1_norm"][0].reshape(8, 128).T
    cv[:, C_G2:C_G2 + 8] = inp["mix_norm"][0].reshape(8, 128).T
    cv[:, C_G3:C_G3 + 8] = inp["ffn2_norm"][0].reshape(8, 128).T
    cv[:, C_GF:C_GF + 8] = inp["final_norm"].reshape(8, 128).T
    cv[:, C_PS:C_PS + 4] = inp["pool_scale"][0].reshape(4, 128).T
    cv[:, C_BG:C_BG + 16] = inp["b_gate"][0].reshape(16, 128).T
    rb = inp["rel_bias"][0]
    cv[:, C_CB:C_CB + 8] = np.broadcast_to(rb[:, 128][None, :], (128, 8))
    k = np.arange(128)[:, None]
    q = np.arange(128)[None, :]
    bt = np.empty((128, 8, 256), np.float32)
    for t in range(2):
        idx = np.clip(128 * t + q - k, -64, 64) + 64
        bt[:, :, t * 128:(t + 1) * 128] = rb[:, idx].transpose(1, 0, 2)
    return cv, bt.reshape(128, 2048)


def _core_consts(t0):
    msk = np.zeros((128, 256), np.float32)
    msk[64:128, 0:64] = NEG
    invc = np.empty((128, 4, 16), np.float32)
    tpos = t0 + np.arange(16)
    for g, w in enumerate(WINDOWS):
        invc[:, g, :] = (1.0 / np.minimum(tpos + 1, w))[None, :]
    hflag = np.full((128, 64), 1.0 if t0 > 0 else 0.0, np.float32)
    ones = np.ones((128, 64), np.float32)
    return np.concatenate([msk, invc.reshape(128, 64), hflag, ones], axis=1)


class Buf:
    __slots__ = ("name", "w", "r")

    def __init__(self, name):
        self.name, self.w, self.r = name, None, []


class Prog:
    ENGS = ("pe", "act", "dve", "pool", "sp")

    def __init__(self):
        self.ops = {e: [] for e in self.ENGS}
        self.cnt = {}
        self.known = {e: {} for e in self.ENGS}

    def _ev(self, key, amount):
        self.cnt[key] = self.cnt.get(key, 0) + amount
        return (key, self.cnt[key])

    def op(self, eng, fn, reads=(), writes=(), dma=None):
        deps = {}

        def add(ev):
            if ev is not None and deps.get(ev[0], 0) < ev[1]:
                deps[ev[0]] = ev[1]

        for b in reads:
            add(b.w)
        for b in writes:
            add(b.w)
            for e in b.r:
                add(e)
        kn = self.known[eng]
        waits = []
        for k, v in deps.items():
            if k == "pe" and eng == "pe":
                continue
            if kn.get(k, 0) >= v:
                continue
            kn[k] = v
            waits.append((k, v))
        ev = self._ev(dma, 16) if dma is not None else self._ev(eng, 1)
        self.ops[eng].append((fn, waits, ev, dma is not None))
        for b in reads:
            b.r.append(ev)
        for b in writes:
            b.w = ev
            b.r = []
        return ev


def handoff(src, dst):
    for d in dst:
        for s in src:
            d.r.extend(s.r)
            if s.w is not None:
                d.r.append(s.w)


def build_nc(stage=4):
    nc = bass.Bass("TRN2", target_bir_lowering=False)
    xT = nc.dram_tensor("xT", [D, TLOC], F32, kind="ExternalInput").ap()
    wst = nc.dram_tensor("wst", [128, WST_TOT], F32, kind="ExternalInput").ap()
    cvd = nc.dram_tensor("cv", [128, C_TOT], F32, kind="ExternalInput").ap()
    btd = nc.dram_tensor("bt", [128, 2048], F32, kind="ExternalInput").ap()
    ccd = nc.dram_tensor("cc", [128, 448], F32, kind="ExternalInput").ap()
    outT = nc.dram_tensor("outT", [D, OWN], F32, kind="ExternalOutput").ap()
    xTv = xT.rearrange("(c p) t -> p c t", p=128)
    outv = outT.rearrange("(c p) t -> p c t", p=128)

    P = Prog()
    es = contextlib.ExitStack()
    with es:
        def sb(name, shape, dt):
            return es.enter_context(nc.sbuf_tensor(name, shape, dt))

        xs = sb("xs", [128, 8, 1024], F32)
        hs = sb("hs", [128, 8, 1024], BF16)
        AREG = 24704
        areg = sb("areg", [128, AREG], BF16)
        ring = sb("ring", [128, RING_SLOTS, SLOT_ELEMS], BF16)
        Kt = sb("Kt", [128, 4, KT_BLOCKS * 128], BF16)
        Vt = sb("Vt", [128, KT_BLOCKS, VROW], BF16)
        biasT = sb("biasT", [128, 8, 256], F32)
        cvt = sb("cvt", [128, C_TOT], F32)
        epst = sb("epst", [128, 2], F32)
        cct = sb("cct", [128, 448], F32)
        zeros_bf = sb("zeros_bf", [128, 512], BF16)
        rtmp = sb("rtmp", [128, 1024], F32)
        ones_bf = sb("ones_bf", [128, 128], BF16)
        ucarry = sb("ucarry", [128, 4, 16], F32)
        tmp16 = sb("tmp16", [128, 16], F32)
        NTMP = 6
        ftmp = sb("ftmp", [128, NTMP, 512], F32)
        NSQ = 3
        sqt = sb("sqt", [128, NSQ, 512], BF16)
        NP = 4
        Pt = sb("Pt", [128, NP, 640], BF16)
        ps = es.enter_context(nc.psum_tensor("ps", [128, 4096], F32))

        mskt = cct[:, 0:256]
        invct = cct[:, 256:320].rearrange("p (g t) -> p g t", g=4)
        hflagt = cct[:, 320:384]
        ones64 = cct[:, 384:448]

        ASUB = NJ * 512

        def a_ap(j, s):
            return areg[:, s * ASUB + j * 512: s * ASUB + (j + 1) * 512]

        def fo_ap(s):
            return areg[:, s * ASUB: s * ASUB + 8192].bitcast(F32).rearrange("p (c t) -> p c t", c=8)
        UPW = 1040
        up32 = areg[:, 0:8320].bitcast(F32).rearrange("p (g t) -> p g t", g=4)
        Qt = areg[:, 8320:12416].rearrange("p (c t) -> p c t", c=4)
        MX = areg[:, 12416:16512].rearrange("p (c t) -> p c t", c=4)
        YP = areg[:, 16512:20608].rearrange("p (c t) -> p c t", c=4)
        ATT = areg[:, 20608:24704].rearrange("p (c t) -> p c t", c=4)
        pscr = areg[:, 16512:16512 + 4160].bitcast(F32).rearrange("p (k t) -> p k t", k=2)
        MG = areg[:, 0:8192].rearrange("p (c t) -> p c t", c=8)
        vw = areg[:, 12416:12416 + 4096].bitcast(F32).rearrange("p (k t) -> p k t", k=4)

        B_x = [[Buf(f"x{m}{s}") for s in range(2)] for m in range(8)]
        B_h = [[Buf(f"h{k}{s}") for s in range(2)] for k in range(8)]
        B_a = [[Buf(f"a{j}{s}") for s in range(2)] for j in range(NJ)]
        B_ring = [Buf(f"ring{i}") for i in range(RING_SLOTS)]
        B_bank = [Buf(f"bank{i}") for i in range(8)]
        B_tmp = [Buf(f"tmp{i}") for i in range(NTMP)]
        B_sq = [Buf(f"sq{i}") for i in range(NSQ)]
        B_P = [Buf(f"P{i}") for i in range(NP)]
        B_Pb = [Buf(f"Pb{i}") for i in range(NP)]
        B_k = [[Buf(f"k{c}{g}") for g in range(3)] for c in range(4)]
        B_v = [Buf(f"v{i}") for i in range(KT_BLOCKS)]
        B_up = [[Buf(f"up{g}{s}") for s in range(2)] for g in range(4)]
        B_upre = [Buf(f"upre{g}") for g in range(4)]
        B_q = [[Buf(f"q{c}{s}") for s in range(2)] for c in range(4)]
        B_mx = [Buf(f"mx{g}") for g in range(4)]
        B_yp = [[Buf(f"yp{g}{s}") for s in range(2)] for g in range(4)]
        B_att = [[Buf(f"att{qi}{par}") for par in range(2)] for qi in range(8)]
        B_pscr = [Buf("pscr0"), Buf("pscr1")]
        B_attp = [[Buf(f"attp{p}{par}") for par in range(2)] for p in range(4)]
        B_mg = [[Buf(f"mg{m}{s}") for s in range(2)] for m in range(8)]
        B_vw = [Buf(f"vw{i}") for i in range(4)]
        B_ucarry = Buf("ucarry")
        B_t16 = Buf("t16")
        B_const = Buf("const")
        B_bias = Buf("bias")
        B_g32 = Buf("g32")
        B_ones = Buf("ones")
        B_zeros = Buf("zeros")
        B_rtmp = [Buf("rtmp0"), Buf("rtmp1")]

        def flat(ll):
            return [b for l in ll for b in l]

        mixer_bufs = (flat(B_up) + B_upre + flat(B_q) + B_mx + flat(B_yp) + flat(B_att) + flat(B_attp)
                      + B_pscr + flat(B_mg) + B_vw)

        rot = {"bank": 0, "tmp": 0, "sq": 0, "P": 0, "ring": 0, "spair": 0, "vw": 0}

        def nxt(kind, n):
            i = rot[kind]
            rot[kind] = (i + 1) % n
            return i

        def bank_ap(b, n=512, off=0):
            return ps[:, b * 512 + off: b * 512 + off + n]

        def dma_fn(out, in_):
            return lambda e: e.dma_start(out=out, in_=in_)

        P.op("sp", dma_fn(cvt[:], cvd), writes=[B_const], dma="cst")
        P.op("sp", dma_fn(biasT[:].rearrange("p h q -> p (h q)"), btd), writes=[B_bias], dma="cst")
        P.op("sp", dma_fn(cct[:], ccd), writes=[Buf("cc")], dma="cst")
        full_cst = ("cst", P.cnt["cst"])
        B_const.w = full_cst
        B_bias.w = full_cst

        P.op("dve", lambda e: e.memset(epst[:], EPS), writes=[B_g32])
        P.op("dve", lambda e: e.memset(ones_bf[:], 1.0), writes=[B_ones])
        P.op("dve", lambda e: e.memset(zeros_bf[:], 0.0), writes=[B_zeros])
        P.op("dve", lambda e: e.tensor_tensor(out=biasT[:], in0=biasT[:],
                                              in1=mskt.unsqueeze(1).to_broadcast([128, 8, 256]), op=ALU.add),
             reads=[B_const], writes=[B_bias])

        nslot = [0]

        def load_slot(name):
            o, n = SLOT_OFF[name]
            i = nxt("ring", RING_SLOTS)
            nslot[0] += 1
            extra = [B_x[0][0]] if 2 <= nslot[0] <= RING_SLOTS else []
            P.op("pool", dma_fn(ring[:, i, 0:n], wst[:, o:o + n]), reads=extra, writes=[B_ring[i]], dma=f"ring{i}")
            return i

        def mm_group(out_ap, pairs, reads, bank):
            n = len(pairs)

            def fn(t):
                ins = None
                for i, (l, r) in enumerate(pairs):
                    ins = t.matmul(out_ap, l, r, start=(i == 0), stop=(i == n - 1))
                return ins
            P.op("pe", fn, reads=reads, writes=[B_bank[bank]])

        def norm(s, gcol, final=False):
            cols = slice(s * 512, (s + 1) * 512)
            b = nxt("bank", 8)
            for kc in range(8):
                qi = nxt("sq", NSQ)
                P.op("act", (lambda e, kc=kc, qi=qi: e.activation(out=sqt[:, qi, :], in_=xs[:, kc, cols],
                                                                  func=AF.Square)),
                     reads=[B_x[kc][s]], writes=[B_sq[qi]])

                def fn(t, kc=kc, qi=qi):
                    return t.matmul(bank_ap(b), ones_bf[:], sqt[:, qi, :], start=(kc == 0), stop=(kc == 7))
                P.op("pe", fn, reads=[B_sq[qi], B_ones], writes=[B_bank[b]])
            ti = nxt("tmp", NTMP)
            P.op("act", lambda e: e.activation(out=ftmp[:, ti, :], in_=bank_ap(b), func=AF.Ln,
                                               scale=1.0 / D, bias=epst[:, 0:1]),
                 reads=[B_bank[b], B_g32], writes=[B_tmp[ti]])
            P.op("act", lambda e: e.activation(out=ftmp[:, ti, :], in_=ftmp[:, ti, :], func=AF.Exp, scale=-0.5),
                 reads=[B_tmp[ti]], writes=[B_tmp[ti]])
            for kc in range(8):
                if final:
                    P.op("dve", (lambda e, kc=kc: e.scalar_tensor_tensor(
                        out=fo_ap(s)[:, kc, :], in0=xs[:, kc, cols], scalar=cvt[:, gcol + kc:gcol + kc + 1],
                        in1=ftmp[:, ti, :], op0=ALU.mult, op1=ALU.mult)),
                        reads=[B_x[kc][s], B_tmp[ti], B_const], writes=[B_a[2 * kc][s], B_a[2 * kc + 1][s]])
                else:
                    P.op("dve", (lambda e, kc=kc: e.scalar_tensor_tensor(
                        out=hs[:, kc, cols], in0=xs[:, kc, cols], scalar=cvt[:, gcol + kc:gcol + kc + 1],
                        in1=ftmp[:, ti, :], op0=ALU.mult, op1=ALU.mult)),
                        reads=[B_x[kc][s], B_tmp[ti], B_const], writes=[B_h[kc][s]])

        def ffn(f, nsub, after_last=None):
            for jp in range(NJ // 2):
                sl = load_slot(f"GU{f}_{jp}")
                for jj in range(2):
                    j = 2 * jp + jj
                    for s in range(nsub):
                        cols = slice(s * 512, (s + 1) * 512)
                        bg = nxt("bank", 8)
                        bu = nxt("bank", 8)
                        hreads = [B_h[kc][s] for kc in range(8)] + [B_ring[sl]]
                        for which, bb in ((0, bg), (1, bu)):
                            base = (jj * 2 + which) * 1024
                            mm_group(bank_ap(bb),
                                     [(ring[:, sl, base + kc * 128: base + (kc + 1) * 128], hs[:, kc, cols])
                                      for kc in range(8)], hreads, bb)
                        ti = nxt("tmp", NTMP)
                        P.op("act", (lambda e, bg=bg, ti=ti: e.activation(out=ftmp[:, ti, :], in_=bank_ap(bg),
                                                                          func=AF.Silu)),
                             reads=[B_bank[bg]], writes=[B_tmp[ti]])
                        P.op("dve", (lambda e, bu=bu, ti=ti, j=j, s=s: e.tensor_tensor(
                            out=a_ap(j, s), in0=ftmp[:, ti, :], in1=bank_ap(bu), op=ALU.mult)),
                            reads=[B_tmp[ti], B_bank[bu]], writes=[B_a[j][s]])
            for m in range(8):
                sl = load_slot(f"DN{f}_{m}")
                for s in range(nsub):
                    cols = slice(s * 512, (s + 1) * 512)
                    b = nxt("bank", 8)
                    mm_group(bank_ap(b), [(ring[:, sl, j * 128:(j + 1) * 128], a_ap(j, s)) for j in range(NJ)],
                             [B_a[j][s] for j in range(NJ)] + [B_ring[sl]], b)
                    P.op("dve", (lambda e, b=b, m=m, cols=cols: e.scalar_tensor_tensor(
                        out=xs[:, m, cols], in0=bank_ap(b), scalar=0.5, in1=xs[:, m, cols],
                        op0=ALU.mult, op1=ALU.add)),
                        reads=[B_bank[b]], writes=[B_x[m][s]])
                    if m == 7 and after_last is not None:
                        after_last(s)

        def load_x(tok0, s):
            P.op("sp", dma_fn(xs[:, :, s * 512:(s + 1) * 512], xTv[:, :, tok0 + s * 512: tok0 + (s + 1) * 512]),
                 writes=[B_x[m][s] for m in range(8)], dma=f"xl{s}")

        def store_x(own0, s, from_fo):
            if from_fo:
                P.op("sp", dma_fn(outv[:, :, own0 + s * 512: own0 + (s + 1) * 512], fo_ap(s)),
                     reads=[B_a[j][s] for j in range(16)], dma=f"st{s}")
            else:
                P.op("sp", dma_fn(outv[:, :, own0 + s * 512: own0 + (s + 1) * 512], xs[:, :, s * 512:(s + 1) * 512]),
                     reads=[B_x[m][s] for m in range(8)], dma=f"st{s}")

        def w_in_phase(mt):
            nsub, blk0, full = mt["nsub"], mt["blk0"], mt["full"]
            for cq in ([0, 1, 2] if full else [0, 2]):
                sl = load_slot(f"IN{cq}")
                for c in range(4):
                    for s in range(nsub):
                        cols = slice(s * 512, (s + 1) * 512)
                        b = nxt("bank", 8)
                        mm_group(bank_ap(b), [(ring[:, sl, c * 1024 + kc * 128: c * 1024 + (kc + 1) * 128],
                                               hs[:, kc, cols]) for kc in range(8)],
                                 [B_h[kc][s] for kc in range(8)] + [B_ring[sl]], b)
                        if cq == 0:
                            P.op("act", (lambda e, b=b, c=c, s=s: e.activation(
                                out=up32[:, c, 16 + s * 512: 16 + (s + 1) * 512], in_=bank_ap(b), func=AF.Copy)),
                                reads=[B_bank[b]], writes=[B_up[c][s]])
                        elif cq == 1:
                            P.op("act", (lambda e, b=b, c=c, cols=cols: e.activation(
                                out=Qt[:, c, cols], in_=bank_ap(b), func=AF.Copy, scale=0.125)),
                                reads=[B_bank[b]], writes=[B_q[c][s]])
                        else:
                            gsub = (blk0 // 4 + s) % 3
                            P.op("dve", (lambda e, b=b, c=c, gsub=gsub: e.tensor_copy(
                                out=Kt[:, c, gsub * 512:(gsub + 1) * 512], in_=bank_ap(b))),
                                reads=[B_bank[b]], writes=[B_k[c][gsub]])
            sl = load_slot("INV")
            for s in range(nsub):
                for tb in range(4):
                    gb = blk0 + s * 4 + tb
                    vs = gb % KT_BLOCKS
                    b = nxt("bank", 8)
                    mm_group(bank_ap(b),
                             [(hs[:, kc, s * 512 + tb * 128: s * 512 + (tb + 1) * 128],
                               ring[:, sl, kc * 512:(kc + 1) * 512]) for kc in range(8)],
                             [B_h[kc][s] for kc in range(8)] + [B_ring[sl]], b)
                    vview = Vt[:, vs, :].rearrange("p (a c) -> p a c", c=192)
                    bview = bank_ap(b).rearrange("p (a e d) -> p a e d", a=4, e=2)
                    P.op("dve", (lambda e, vview=vview, bview=bview: e.tensor_copy(
                        out=vview[:, :, 0:64], in_=bview[:, :, 0, :])),
                        reads=[B_bank[b]], writes=[B_v[vs]])
                    P.op("dve", (lambda e, vview=vview, bview=bview: e.tensor_copy(
                        out=vview[:, :, 128:192], in_=bview[:, :, 1, :])),
                        reads=[B_bank[b]], writes=[B_v[vs]])
                    src = ones64 if full else hflagt
                    P.op("act", (lambda e, vview=vview, src=src: e.activation(
                        out=vview[:, :, 64:128], in_=src.unsqueeze(1).to_broadcast([128, 4, 64]), func=AF.Copy)),
                        reads=[B_const], writes=[B_v[vs]])

        def save_carry(ntok):
            P.op("dve", lambda e: e.tensor_copy(out=ucarry[:], in_=up32[:, :, ntok: ntok + 16]),
                 reads=[B_up[g][(ntok // 512) - 1] for g in range(4)], writes=[B_ucarry])

        def pooling(mt):
            ntok = mt["ntok"]
            L = 16 + ntok
            P.op("dve", lambda e: e.tensor_copy(out=up32[:, :, 0:16], in_=ucarry[:]),
                 reads=[B_ucarry], writes=B_upre)
            for g, w in enumerate(WINDOWS):
                src_ap = lambda lo, hi, g=g: up32[:, g, lo:hi]
                src_b = [B_up[g][0], B_up[g][1], B_upre[g]]
                v0 = 0
                k = 0
                st = 1
                while st < w:
                    v = v0 + st
                    dst_ap = (lambda lo, hi, k=k: pscr[:, k, lo:hi])
                    P.op("dve", (lambda e, dst_ap=dst_ap, src_ap=src_ap, v=v, st=st: e.tensor_tensor(
                        out=dst_ap(v, L), in0=src_ap(v, L), in1=src_ap(v - st, L - st), op=ALU.add)),
                        reads=src_b, writes=[B_pscr[k]])
                    src_ap, src_b = dst_ap, [B_pscr[k]]
                    v0 = v
                    k ^= 1
                    st *= 2
                P.op("dve", (lambda e, src_ap=src_ap, g=g, w=w: e.scalar_tensor_tensor(
                    out=MX[:, g, 0:ntok], in0=src_ap(16, L), scalar=1.0 / w, in1=up32[:, g, 16:L],
                    op0=ALU.mult, op1=ALU.subtract)),
                    reads=src_b + [B_up[g][0], B_up[g][1]], writes=[B_mx[g]])
                if mt["first_own"]:
                    P.op("dve", (lambda e, src_ap=src_ap, g=g: e.tensor_tensor(
                        out=tmp16[:], in0=src_ap(16, 32), in1=invct[:, g, :], op=ALU.mult)),
                        reads=src_b + [B_const], writes=[B_t16])
                    P.op("dve", (lambda e, g=g: e.tensor_tensor(
                        out=MX[:, g, 0:16], in0=tmp16[:], in1=up32[:, g, 16:32], op=ALU.subtract)),
                        reads=[B_t16, B_up[g][0]], writes=[B_mx[g]])
            save_carry(ntok)

        def pool_w_phase(mt):
            sl = load_slot("PW")
            for g in range(4):
                for s in range(mt["nsub"]):
                    cols = slice(s * 512, (s + 1) * 512)
                    b = nxt("bank", 8)
                    mm_group(bank_ap(b), [(ring[:, sl, g * 128:(g + 1) * 128], MX[:, g, cols])],
                             [B_mx[g], B_ring[sl]], b)
                    P.op("act", (lambda e, b=b, g=g, cols=cols: e.activation(
                        out=YP[:, g, cols], in_=bank_ap(b), func=AF.Identity,
                        scale=cvt[:, C_PS + g:C_PS + g + 1])),
                        reads=[B_bank[b], B_const], writes=[B_yp[g][s]])

        def attention(mt):
            blk0 = mt["blk0"]
            q_lo, q_hi = blk0, blk0 + 7
            units = []
            for h in range(8):
                for j in range(blk0 - 4, blk0 + 8):
                    units.append((h, j, max(j, q_lo), min(j + 4, q_hi)))
            state = {}
            SB = [(0, 1), (2, 3)]
            OB = [(4, 5), (6, 7)]

            def issue_S(u):
                h, j, qa, qb = units[u]
                p, r0 = h // 2, 64 * (h % 2)
                nq = qb - qa + 1
                N = nq * 128
                si = nxt("spair", 2)
                b0, b1 = SB[si]
                col = (j % KT_BLOCKS) * 128
                ql = (qa - blk0) * 128
                ta, tb = qa - j, qb - j
                nA = max(0, min(tb, 1) - ta + 1) * 128
                nB = N - nA
                segs = []
                if nA:
                    segs.append((b0 * 512, 0, nA))
                if nB:
                    segs.append((b1 * 512, nA, nB))

                def fn(t):
                    ins = None
                    for (pc, qc, n) in segs:
                        ins = t.matmul(ps[:, pc: pc + n], Kt[r0:r0 + 64, p, col:col + 128],
                                       Qt[r0:r0 + 64, p, ql + qc: ql + qc + n], start=True, stop=True)
                    return ins
                wb = ([B_bank[b0]] if nA else []) + ([B_bank[b1]] if nB else [])
                P.op("pe", fn, reads=[B_k[p][(j // 4) % 3]] + [B_q[p][s_] for s_ in sorted({(qa - blk0) // 4, (qb - blk0) // 4})],
                     writes=wb)
                pi = nxt("P", NP)
                cbias = cvt[:, C_CB + h:C_CB + h + 1]
                if nB:
                    P.op("act", (lambda e: e.activation(out=Pt[:, pi, nA:N], in_=ps[:, b1 * 512: b1 * 512 + nB],
                                                        func=AF.Exp, bias=cbias)),
                         reads=[B_bank[b1], B_const], writes=[B_P[pi]])
                    if tb == 4:
                        P.op("dve", (lambda e: e.memset(Pt[0:64, pi, N - 64:N], 0.0)), writes=[B_P[pi]])
                if nA:
                    P.op("dve", (lambda e: e.tensor_tensor(out=ps[:, b0 * 512: b0 * 512 + nA],
                                                           in0=ps[:, b0 * 512: b0 * 512 + nA],
                                                           in1=biasT[:, h, ta * 128: ta * 128 + nA], op=ALU.add)),
                         reads=[B_bias], writes=[B_bank[b0]])
                    P.op("act", (lambda e: e.activation(out=Pt[:, pi, 0:nA], in_=ps[:, b0 * 512: b0 * 512 + nA],
                                                        func=AF.Exp)),
                         reads=[B_bank[b0]], writes=[B_Pb[pi]])
                state[u] = pi

            def issue_PV(u):
                h, j, qa, qb = units[u]
                pi = state.pop(u)
                p, par = h // 2, h % 2
                ob = OB[h % 2]
                vs = j % KT_BLOCKS
                lhsT = Vt[:, vs, p * 192 + 64 * par: p * 192 + 64 * par + 128]
                if j == blk0 - 4:
                    def fz(t):
                        ins = None
                        for b in ob:
                            ins = t.matmul(bank_ap(b), zeros_bf[:, 0:128], zeros_bf[:, 0:512], start=True, stop=False,
                                           skip_group_check=True)
                        return ins
                    P.op("pe", fz, reads=[B_zeros], writes=[B_bank[ob[0]], B_bank[ob[1]]])
                segs = []
                for half in range(2):
                    lo, hi = max(qa, blk0 + 4 * half), min(qb, blk0 + 4 * half + 3)
                    if lo <= hi:
                        segs.append((ob[half], (lo - blk0 - 4 * half) * 128, (lo - qa) * 128, (hi - lo + 1) * 128))

                def fn(t):
                    ins = None
                    for _ in range(NFILL):
                        t.matmul(bank_ap(ob[0]), zeros_bf[:, 0:128], zeros_bf[:, 0:512], start=False, stop=True,
                                 skip_group_check=True)
                    for (b, oc, pc, n) in segs:
                        ins = t.matmul(bank_ap(b, n, oc), lhsT, Pt[:, pi, pc: pc + n], start=False, stop=True,
                                       skip_group_check=True)
                    return ins
                P.op("pe", fn, reads=[B_P[pi], B_Pb[pi], B_v[vs], B_zeros], writes=[B_bank[b] for (b, _, _, _) in segs] + [B_bank[ob[0]]])
                if j == blk0 + 7:
                    o0, d0 = 64 * par, 64 * (1 - par)
                    oall = ps[:, ob[0] * 512: ob[0] * 512 + 1024]
                    P.op("act", (lambda e: e.activation(out=rtmp[o0:o0 + 64, :], in_=oall[d0:d0 + 64, :], func=AF.Ln)),
                         reads=[B_bank[ob[0]], B_bank[ob[1]]], writes=[B_rtmp[par]])
                    P.op("act", (lambda e: e.activation(out=rtmp[o0:o0 + 64, :], in_=rtmp[o0:o0 + 64, :], func=AF.Exp,
                                                        scale=-1.0)),
                         reads=[B_rtmp[par]], writes=[B_rtmp[par]])
                    P.op("dve", (lambda e: e.tensor_tensor(out=ATT[o0:o0 + 64, p, :], in0=oall[o0:o0 + 64, :],
                                                           in1=rtmp[o0:o0 + 64, :], op=ALU.mult)),
                         reads=[B_bank[ob[0]], B_bank[ob[1]], B_rtmp[par]],
                         writes=[B_attp[p][par]])

            LAG = 1
            NFILL = 1
            n = len(units)
            for u in range(n + LAG):
                if u < n:
                    issue_S(u)
                if u - LAG >= 0:
                    issue_PV(u - LAG)

        def gates_phase(mt):
            for m in range(8):
                sl = load_slot(f"GB{m}")
                for s in range(mt["nsub"]):
                    cols = slice(s * 512, (s + 1) * 512)
                    bA, bB, bP, bT = (nxt("bank", 8) for _ in range(4))
                    hreads = [B_h[kc][s] for kc in range(8)] + [B_ring[sl]]
                    mm_group(bank_ap(bA), [(ring[:, sl, kc * 128:(kc + 1) * 128], hs[:, kc, cols])
                                           for kc in range(8)], hreads, bA)
                    mm_group(bank_ap(bB), [(ring[:, sl, 1024 + kc * 128: 1024 + (kc + 1) * 128], hs[:, kc, cols])
                                           for kc in range(8)], hreads, bB)
                    mm_group(bank_ap(bP), [(ring[:, sl, 2048 + kc * 128: 2048 + (kc + 1) * 128], YP[:, kc, cols])
                                           for kc in range(4)], [B_yp[g][s] for g in range(4)] + [B_ring[sl]], bP)
                    mm_group(bank_ap(bT), [(ring[:, sl, 2560 + kc * 128: 2560 + (kc + 1) * 128], ATT[:, kc, cols])
                                           for kc in range(4)],
                             flat(B_attp) + [B_ring[sl]], bT)
                    tA, tB = nxt("tmp", NTMP), nxt("tmp", NTMP)
                    P.op("act", (lambda e, bA=bA, tA=tA, m=m: e.activation(
                        out=ftmp[:, tA, :], in_=bank_ap(bA), func=AF.Sigmoid, bias=cvt[:, C_BG + m:C_BG + m + 1])),
                        reads=[B_bank[bA], B_const], writes=[B_tmp[tA]])
                    P.op("act", (lambda e, bB=bB, tB=tB, m=m: e.activation(
                        out=ftmp[:, tB, :], in_=bank_ap(bB), func=AF.Sigmoid,
                        bias=cvt[:, C_BG + 8 + m:C_BG + 8 + m + 1])),
                        reads=[B_bank[bB], B_const], writes=[B_tmp[tB]])
                    v1, v2 = nxt("vw", 4), nxt("vw", 4)
                    P.op("dve", (lambda e, tA=tA, bP=bP, v1=v1: e.tensor_tensor(
                        out=vw[:, v1, :], in0=ftmp[:, tA, :], in1=bank_ap(bP), op=ALU.mult)),
                        reads=[B_tmp[tA], B_bank[bP]], writes=[B_vw[v1]])
                    P.op("dve", (lambda e, tB=tB, bT=bT, v2=v2: e.tensor_tensor(
                        out=vw[:, v2, :], in0=ftmp[:, tB, :], in1=bank_ap(bT), op=ALU.mult)),
                        reads=[B_tmp[tB], B_bank[bT]], writes=[B_vw[v2]])
                    P.op("dve", (lambda e, v1=v1, v2=v2, m=m, cols=cols: e.tensor_tensor(
                        out=MG[:, m, cols], in0=vw[:, v1, :], in1=vw[:, v2, :], op=ALU.add)),
                        reads=[B_vw[v1], B_vw[v2]], writes=[B_mg[m][s]])

        def w_out_phase(mt, after_last=None):
            for mq in range(2):
                sl = load_slot(f"WO{mq}")
                for mm in range(4):
                    m = 4 * mq + mm
                    for s in range(mt["nsub"]):
                        cols = slice(s * 512, (s + 1) * 512)
                        b = nxt("bank", 8)
                        mm_group(bank_ap(b), [(ring[:, sl, mm * 1024 + kc * 128: mm * 1024 + (kc + 1) * 128],
                                               MG[:, kc, cols]) for kc in range(8)],
                                 [B_mg[kc][s] for kc in range(8)] + [B_ring[sl]], b)
                        P.op("dve", (lambda e, b=b, m=m, cols=cols: e.tensor_tensor(
                            out=xs[:, m, cols], in0=bank_ap(b), in1=xs[:, m, cols], op=ALU.add)),
                            reads=[B_bank[b]], writes=[B_x[m][s]])
                        if m == 7 and after_last is not None:
                            after_last(s)

        MTS = [dict(name="H", tok0=0, ntok=512, nsub=1, full=False, blk0=0, first_own=False),
               dict(name="A", tok0=512, ntok=1024, nsub=2, full=True, blk0=4, first_own=True),
               dict(name="B", tok0=1536, ntok=1024, nsub=2, full=True, blk0=12, first_own=False)]

        load_x(MTS[0]["tok0"], 0)
        load_x(MTS[1]["tok0"], 1)
        for im, mt in enumerate(MTS):
            nsub, full = mt["nsub"], mt["full"]
            nxt_mt = MTS[im + 1] if im + 1 < len(MTS) else None
            for s in range(nsub):
                norm(s, C_G1)
            if stage >= 2:
                def after_ffn1(s, mt=mt, nxt_mt=nxt_mt):
                    norm(s, C_G2)
                    if not mt["full"]:
                        load_x(nxt_mt["tok0"], s)
                ffn(1, nsub, after_last=after_ffn1)
            else:
                ffn(1, nsub)
            if stage >= 2:
                handoff(flat(B_a), mixer_bufs)
                w_in_phase(mt)
                if not full:
                    save_carry(mt["ntok"])
                else:
                    pooling(mt)
                    handoff(B_pscr, flat(B_yp) + flat(B_att) + flat(B_attp))
                    attention(mt)
                    pool_w_phase(mt)
                    handoff(flat(B_up) + B_upre + flat(B_q), flat(B_mg))
                    handoff(B_mx, B_vw)
                    gates_phase(mt)
                    if stage >= 3:
                        w_out_phase(mt, after_last=lambda s: norm(s, C_G3))
                    else:
                        w_out_phase(mt)
                handoff(mixer_bufs, flat(B_a))
            if full:
                own0 = mt["tok0"] - HALO
                if stage >= 4:
                    def after_ffn2(s, own0=own0, nxt_mt=nxt_mt):
                        norm(s, C_GF, final=True)
                        store_x(own0, s, True)
                        if nxt_mt is not None:
                            load_x(nxt_mt["tok0"], s)
                    ffn(2, nsub, after_last=after_ffn2)
                else:
                    if stage >= 3:
                        ffn(2, nsub)
                    for s in range(nsub):
                        store_x(own0, s, False)
                        if nxt_mt is not None:
                            load_x(nxt_mt["tok0"], s)

        sems = {}
        for key in P.cnt:
            sems[key] = es.enter_context(nc.semaphore(f"s_{key}"))
        for e in ("pe", "act", "dve"):
            if e not in sems:
                sems[e] = es.enter_context(nc.semaphore(f"s_{e}"))

        def emit(engname, eng, tail=None):
            for fn, waits, ev, is_dma in P.ops[engname]:
                for k, v in waits:
                    eng.wait_ge(sems[k], v)
                ins = fn(eng)
                ins.then_inc(sems[ev[0]], 16 if is_dma else 1)
            if tail is not None:
                tail(eng)

        with nc.Block() as block:
            @block.sync
            def _(sync):
                def tail(e):
                    for k in ("st0", "st1"):
                        e.wait_ge(sems[k], P.cnt[k])
                emit("sp", sync, tail)

            @block.gpsimd
            def _(g):
                emit("pool", g)

            @block.tensor
            def _(t):
                emit("pe", t)

            @block.scalar
            def _(s):
                emit("act", s)

            @block.vector
            def _(v):
                emit("dve", v)
    return nc


_CACHE = {}


def kernel(**inputs):
    stage = int(inputs.pop("_stage", 4))
    inp = {k: np.asarray(v, dtype=np.float32) for k, v in inputs.items()}
    x = inp["x"]
    wst = _build_wst(inp)
    cv, bt = _build_consts(inp)
    in_maps = []
    for c in range(NCORES):
        b, t0 = c // 4, (c % 4) * OWN
        xT = np.zeros((D, TLOC), np.float32)
        xT[:, HALO:] = x[b, t0:t0 + OWN].T
        if t0 > 0:
            xT[:, :HALO] = x[b, t0 - HALO:t0].T
        in_maps.append({"xT": xT, "wst": wst, "cv": cv, "bt": bt, "cc": _core_consts(t0)})
    if stage not in _CACHE:
        _CACHE[stage] = build_nc(stage)
    res = run_bass_kernel_spmd(_CACHE[stage], in_maps, core_ids=list(range(NCORES)))
    out = np.empty((2, SEQ, D), np.float32)
    for c in range(NCORES):
        b, t0 = c // 4, (c % 4) * OWN
        out[b, t0:t0 + OWN] = res.results[c]["outT"].T
    return out
```

```python
import contextlib
import numpy as np
import concourse.bass as bass
import concourse.mybir as mybir
from concourse.bass_utils import run_bass_kernel_spmd

F32 = mybir.dt.float32
BF16 = mybir.dt.bfloat16
AF = mybir.ActivationFunctionType
ALU = mybir.AluOpType

D = 1024
DFF = 2816
NJ = DFF // 128
SEQ = 8192
NCORES = 8
OWN = 2048
HALO = 512
TLOC = OWN + HALO
EPS = 1e-6
NEG = -30000.0
WINDOWS = (2, 4, 8, 16)

RING_SLOTS = 5
SLOT_ELEMS = 4096
KT_BLOCKS = 12
VROW = 768

SLOT_SIZES = {}
SLOT_ORDER = []


def _def_slots():
    for f in (1, 2):
        for jp in range(NJ // 2):
            SLOT_SIZES[f"GU{f}_{jp}"] = 4096
        for m in range(8):
            SLOT_SIZES[f"DN{f}_{m}"] = DFF
    for cq in range(3):
        SLOT_SIZES[f"IN{cq}"] = 4096
    SLOT_SIZES["INV"] = 4096
    SLOT_SIZES["PW"] = 512
    for m in range(8):
        SLOT_SIZES[f"GB{m}"] = 3072
    for mq in range(2):
        SLOT_SIZES[f"WO{mq}"] = 4096
    off = 0
    for k, v in SLOT_SIZES.items():
        SLOT_ORDER.append((k, off, v))
        off += v
    return off


WST_TOT = _def_slots()
SLOT_OFF = {k: (o, n) for k, o, n in SLOT_ORDER}

C_G1, C_G2, C_G3, C_GF, C_PS, C_BG, C_CB = 0, 8, 16, 24, 32, 36, 52
C_TOT = 64


def _colchunk(W, c0, ncol=128):
    K = W.shape[0]
    return W[:, c0:c0 + ncol].reshape(K // 128, 128, ncol).transpose(1, 0, 2).reshape(128, -1)


def _build_wst(inp):
    wst = np.empty((128, WST_TOT), np.float32)

    def put(name, arr):
        o, n = SLOT_OFF[name]
        assert arr.shape == (128, n), (name, arr.shape, n)
        wst[:, o:o + n] = arr

    for f, pre in ((1, "ffn1"), (2, "ffn2")):
        wg, wu, wd = inp[pre + "_w_gate"][0], inp[pre + "_w_up"][0], inp[pre + "_w_down"][0]
        for jp in range(NJ // 2):
            parts = []
            for jj in range(2):
                j = 2 * jp + jj
                parts += [_colchunk(wg, j * 128), _colchunk(wu, j * 128)]
            put(f"GU{f}_{jp}", np.concatenate(parts, axis=1))
        for m in range(8):
            put(f"DN{f}_{m}", _colchunk(wd, m * 128))
    w_in = inp["w_in"][0]
    for cq in range(3):
        put(f"IN{cq}", np.concatenate([_colchunk(w_in, (4 * cq + c) * 128) for c in range(4)], axis=1))
    put("INV", _colchunk(w_in, 1536, 512))
    put("PW", inp["pool_w"][0].transpose(1, 0, 2).reshape(128, 512))
    wgt, wbp, wba = inp["w_gate"][0], inp["w_branch_pool"][0], inp["w_branch_attn"][0]
    for m in range(8):
        put(f"GB{m}", np.concatenate([_colchunk(wgt, m * 128), _colchunk(wgt, (8 + m) * 128),
                                      _colchunk(wbp, m * 128), _colchunk(wba, m * 128)], axis=1))
    w_out = inp["w_out"][0]
    for mq in range(2):
        put(f"WO{mq}", np.concatenate([_colchunk(w_out, (4 * mq + c) * 128) for c in range(4)], axis=1))
    return wst


def _build_consts(inp):
    cv = np.zeros((128, C_TOT), np.float32)
    cv[:, C_G1:C_G1 + 8] = inp["ffn1_norm"][0].reshape(8, 128).T
    cv[:, C_G2:C_G2 + 8] = inp["mix_norm"][0].reshape(8, 128).T
    cv[:, C_G3:C_G3 + 8] = inp["ffn2_norm"][0].reshape(8, 128).T
    cv[:, C_GF:C_GF + 8] = inp["final_norm"].reshape(8, 128).T
    cv[:, C_PS:C_PS + 4] = inp["pool_scale"][0].reshape(4, 128).T
    cv[:, C_BG:C_BG + 16] = inp["b_gate"][0].reshape(16, 128).T
    rb = inp["rel_bias"][0]
    cv[:, C_CB:C_CB + 8] = np.broadcast_to(rb[:, 128][None, :], (128, 8))
    k = np.arange(128)[:, None]
    q = np.arange(128)[None, :]
    bt = np.empty((128, 8, 256), np.float32)
    for t in range(2):
        idx = np.clip(128 * t + q - k, -64, 64) + 64
        bt[:, :, t * 128:(t + 1) * 128] = rb[:, idx].transpose(1, 0, 2)
    return cv, bt.reshape(128, 2048)


def _core_consts(t0):
    msk = np.zeros((128, 256), np.float32)
    msk[64:128, 0:64] = NEG
    invc = np.empty((128, 4, 16), np.float32)
    tpos = t0 + np.arange(16)
    for g, w in enumerate(WINDOWS):
        invc[:, g, :] = (1.0 / np.minimum(tpos + 1, w))[None, :]
    hflag = np.full((128, 64), 1.0 if t0 > 0 else 0.0, np.float32)
    ones = np.ones((128, 64), np.float32)
    return np.concatenate([msk, invc.reshape(128, 64), hflag, ones], axis=1)


class Buf:
    __slots__ = ("name", "w", "r")

    def __init__(self, name):
        self.name, self.w, self.r = name, None, []


class Prog:
    ENGS = ("pe", "act", "dve", "pool", "sp")

    def __init__(self):
        self.ops = {e: [] for e in self.ENGS}
        self.cnt = {}
        self.known = {e: {} for e in self.ENGS}

    def _ev(self, key, amount):
        self.cnt[key] = self.cnt.get(key, 0) + amount
        return (key, self.cnt[key])

    def op(self, eng, fn, reads=(), writes=(), dma=None):
        deps = {}

        def add(ev):
            if ev is not None and deps.get(ev[0], 0) < ev[1]:
                deps[ev[0]] = ev[1]

        for b in reads:
            add(b.w)
        for b in writes:
            add(b.w)
            for e in b.r:
                add(e)
        kn = self.known[eng]
        waits = []
        for k, v in deps.items():
            if k == "pe" and eng == "pe":
                continue
            if kn.get(k, 0) >= v:
                continue
            kn[k] = v
            waits.append((k, v))
        ev = self._ev(dma, 16) if dma is not None else self._ev(eng, 1)
        self.ops[eng].append((fn, waits, ev, dma is not None))
        for b in reads:
            b.r.append(ev)
        for b in writes:
            b.w = ev
            b.r = []
        return ev


def handoff(src, dst):
    for d in dst:
        for s in src:
            d.r.extend(s.r)
            if s.w is not None:
                d.r.append(s.w)


def build_nc(stage=4):
    nc = bass.Bass("TRN2", target_bir_lowering=False)
    xT = nc.dram_tensor("xT", [D, TLOC], F32, kind="ExternalInput").ap()
    wst = nc.dram_tensor("wst", [128, WST_TOT], F32, kind="ExternalInput").ap()
    cvd = nc.dram_tensor("cv", [128, C_TOT], F32, kind="ExternalInput").ap()
    btd = nc.dram_tensor("bt", [128, 2048], F32, kind="ExternalInput").ap()
    ccd = nc.dram_tensor("cc", [128, 448], F32, kind="ExternalInput").ap()
    outT = nc.dram_tensor("outT", [D, OWN], F32, kind="ExternalOutput").ap()
    xTv = xT.rearrange("(c p) t -> p c t", p=128)
    outv = outT.rearrange("(c p) t -> p c t", p=128)

    P = Prog()
    es = contextlib.ExitStack()
    with es:
        def sb(name, shape, dt):
            return es.enter_context(nc.sbuf_tensor(name, shape, dt))

        xs = sb("xs", [128, 8, 1024], F32)
        hs = sb("hs", [128, 8, 1024], BF16)
        AREG = 24704
        areg = sb("areg", [128, AREG], BF16)
        ring = sb("ring", [128, RING_SLOTS, SLOT_ELEMS], BF16)
        Kt = sb("Kt", [128, 4, KT_BLOCKS * 128], BF16)
        Vt = sb("Vt", [128, KT_BLOCKS, VROW], BF16)
        biasT = sb("biasT", [128, 8, 256], F32)
        cvt = sb("cvt", [128, C_TOT], F32)
        epst = sb("epst", [128, 2], F32)
        cct = sb("cct", [128, 448], F32)
        zeros_bf = sb("zeros_bf", [128, 512], BF16)
        rtmp = sb("rtmp", [128, 1024], F32)
        ones_bf = sb("ones_bf", [128, 128], BF16)
        ucarry = sb("ucarry", [128, 4, 16], F32)
        tmp16 = sb("tmp16", [128, 16], F32)
        NTMP = 6
        ftmp = sb("ftmp", [128, NTMP, 512], F32)
        NSQ = 3
        sqt = sb("sqt", [128, NSQ, 512], BF16)
        NP = 4
        Pt = sb("Pt", [128, NP, 640], BF16)
        ps = es.enter_context(nc.psum_tensor("ps", [128, 4096], F32))

        pscr = ftmp[:].rearrange("p a b -> p (a b)")[:, 0:2080].rearrange("p (k t) -> p k t", k=2)
        mskt = cct[:, 0:256]
        invct = cct[:, 256:320].rearrange("p (g t) -> p g t", g=4)
        hflagt = cct[:, 320:384]
        ones64 = cct[:, 384:448]

        ASUB = NJ * 512

        def a_ap(j, s):
            return areg[:, s * ASUB + j * 512: s * ASUB + (j + 1) * 512]

        def fo_ap(s):
            return areg[:, s * ASUB: s * ASUB + 8192].bitcast(F32).rearrange("p (c t) -> p c t", c=8)
        UPW = 1040
        up32 = areg[:, 0:8320].bitcast(F32).rearrange("p (g t) -> p g t", g=4)
        Qt = areg[:, 8320:12416].rearrange("p (c t) -> p c t", c=4)
        MX = areg[:, 12416:16512].rearrange("p (c t) -> p c t", c=4)
        YP = areg[:, 16512:20608].rearrange("p (c t) -> p c t", c=4)
        ATT = areg[:, 20608:24704].rearrange("p (c t) -> p c t", c=4)
        MG = areg[:, 0:8192].rearrange("p (c t) -> p c t", c=8)
        vw = areg[:, 12416:12416 + 4096].bitcast(F32).rearrange("p (k t) -> p k t", k=4)

        B_x = [[Buf(f"x{m}{s}") for s in range(2)] for m in range(8)]
        B_h = [[Buf(f"h{k}{s}") for s in range(2)] for k in range(8)]
        B_a = [[Buf(f"a{j}{s}") for s in range(2)] for j in range(NJ)]
        B_ring = [Buf(f"ring{i}") for i in range(RING_SLOTS)]
        B_bank = [Buf(f"bank{i}") for i in range(8)]
        B_tmp = [Buf(f"tmp{i}") for i in range(NTMP)]
        B_sq = [Buf(f"sq{i}") for i in range(NSQ)]
        B_P = [Buf(f"P{i}") for i in range(NP)]
        B_Pb = [Buf(f"Pb{i}") for i in range(NP)]
        B_k = [[Buf(f"k{c}{g}") for g in range(3)] for c in range(4)]
        B_v = [Buf(f"v{i}") for i in range(KT_BLOCKS)]
        B_up = [[Buf(f"up{g}{s}") for s in range(2)] for g in range(4)]
        B_upre = [Buf(f"upre{g}") for g in range(4)]
        B_q = [[Buf(f"q{c}{s}") for s in range(2)] for c in range(4)]
        B_mx = [Buf(f"mx{g}") for g in range(4)]
        B_yp = [[Buf(f"yp{g}{s}") for s in range(2)] for g in range(4)]
        B_att = [[Buf(f"att{qi}{par}") for par in range(2)] for qi in range(8)]
        B_pscr = [Buf("pscr0"), Buf("pscr1")]
        B_attp = [[Buf(f"attp{p}{par}") for par in range(2)] for p in range(4)]
        B_mg = [[Buf(f"mg{m}{s}") for s in range(2)] for m in range(8)]
        B_vw = [Buf(f"vw{i}") for i in range(4)]
        B_ucarry = Buf("ucarry")
        B_t16 = Buf("t16")
        B_const = Buf("const")
        B_bias = Buf("bias")
        B_g32 = Buf("g32")
        B_ones = Buf("ones")
        B_zeros = Buf("zeros")
        B_rtmp = [Buf("rtmp0"), Buf("rtmp1")]

        def flat(ll):
            return [b for l in ll for b in l]

        mixer_bufs = (flat(B_up) + B_upre + flat(B_q) + B_mx + flat(B_yp) + flat(B_att) + flat(B_attp)
                      + B_pscr + flat(B_mg) + B_vw)

        rot = {"bank": 0, "tmp": 0, "sq": 0, "P": 0, "ring": 0, "spair": 0, "vw": 0}

        def nxt(kind, n):
            i = rot[kind]
            rot[kind] = (i + 1) % n
            return i

        def bank_ap(b, n=512, off=0):
            return ps[:, b * 512 + off: b * 512 + off + n]

        def dma_fn(out, in_):
            return lambda e: e.dma_start(out=out, in_=in_)

        P.op("sp", dma_fn(cvt[:], cvd), writes=[B_const], dma="cst")
        P.op("sp", dma_fn(biasT[:].rearrange("p h q -> p (h q)"), btd), writes=[B_bias], dma="cst")
        P.op("sp", dma_fn(cct[:], ccd), writes=[Buf("cc")], dma="cst")
        full_cst = ("cst", P.cnt["cst"])
        B_const.w = full_cst
        B_bias.w = full_cst

        P.op("dve", lambda e: e.memset(epst[:], EPS), writes=[B_g32])
        P.op("dve", lambda e: e.memset(ones_bf[:], 1.0), writes=[B_ones])
        P.op("dve", lambda e: e.memset(zeros_bf[:], 0.0), writes=[B_zeros])
        P.op("dve", lambda e: e.tensor_tensor(out=biasT[:], in0=biasT[:],
                                              in1=mskt.unsqueeze(1).to_broadcast([128, 8, 256]), op=ALU.add),
             reads=[B_const], writes=[B_bias])

        nslot = [0]

        def load_slot(name):
            o, n = SLOT_OFF[name]
            i = nxt("ring", RING_SLOTS)
            nslot[0] += 1
            extra = [B_x[0][0]] if 2 <= nslot[0] <= RING_SLOTS else []
            P.op("pool", dma_fn(ring[:, i, 0:n], wst[:, o:o + n]), reads=extra, writes=[B_ring[i]], dma=f"ring{i}")
            return i

        def mm_group(out_ap, pairs, reads, bank):
            n = len(pairs)

            def fn(t):
                ins = None
                for i, (l, r) in enumerate(pairs):
                    ins = t.matmul(out_ap, l, r, start=(i == 0), stop=(i == n - 1))
                return ins
            P.op("pe", fn, reads=reads, writes=[B_bank[bank]])

        def norm(s, gcol, final=False):
            cols = slice(s * 512, (s + 1) * 512)
            b = nxt("bank", 8)
            for kc in range(8):
                qi = nxt("sq", NSQ)
                P.op("act", (lambda e, kc=kc, qi=qi: e.activation(out=sqt[:, qi, :], in_=xs[:, kc, cols],
                                                                  func=AF.Square)),
                     reads=[B_x[kc][s]], writes=[B_sq[qi]])

                def fn(t, kc=kc, qi=qi):
                    return t.matmul(bank_ap(b), ones_bf[:], sqt[:, qi, :], start=(kc == 0), stop=(kc == 7))
                P.op("pe", fn, reads=[B_sq[qi], B_ones], writes=[B_bank[b]])
            ti = nxt("tmp", NTMP)
            P.op("act", lambda e: e.activation(out=ftmp[:, ti, :], in_=bank_ap(b), func=AF.Ln,
                                               scale=1.0 / D, bias=epst[:, 0:1]),
                 reads=[B_bank[b], B_g32], writes=[B_tmp[ti]])
            P.op("act", lambda e: e.activation(out=ftmp[:, ti, :], in_=ftmp[:, ti, :], func=AF.Exp, scale=-0.5),
                 reads=[B_tmp[ti]], writes=[B_tmp[ti]])
            for kc in range(8):
                if final:
                    P.op("dve", (lambda e, kc=kc: e.scalar_tensor_tensor(
                        out=fo_ap(s)[:, kc, :], in0=xs[:, kc, cols], scalar=cvt[:, gcol + kc:gcol + kc + 1],
                        in1=ftmp[:, ti, :], op0=ALU.mult, op1=ALU.mult)),
                        reads=[B_x[kc][s], B_tmp[ti], B_const], writes=[B_a[2 * kc][s], B_a[2 * kc + 1][s]])
                else:
                    P.op("dve", (lambda e, kc=kc: e.scalar_tensor_tensor(
                        out=hs[:, kc, cols], in0=xs[:, kc, cols], scalar=cvt[:, gcol + kc:gcol + kc + 1],
                        in1=ftmp[:, ti, :], op0=ALU.mult, op1=ALU.mult)),
                        reads=[B_x[kc][s], B_tmp[ti], B_const], writes=[B_h[kc][s]])

        def ffn(f, nsub, after_last=None):
            for jp in range(NJ // 2):
                sl = load_slot(f"GU{f}_{jp}")
                for jj in range(2):
                    j = 2 * jp + jj
                    for s in range(nsub):
                        cols = slice(s * 512, (s + 1) * 512)
                        bg = nxt("bank", 8)
                        bu = nxt("bank", 8)
                        hreads = [B_h[kc][s] for kc in range(8)] + [B_ring[sl]]
                        for which, bb in ((0, bg), (1, bu)):
                            base = (jj * 2 + which) * 1024
                            mm_group(bank_ap(bb),
                                     [(ring[:, sl, base + kc * 128: base + (kc + 1) * 128], hs[:, kc, cols])
                                      for kc in range(8)], hreads, bb)
                        ti = nxt("tmp", NTMP)
                        P.op("act", (lambda e, bg=bg, ti=ti: e.activation(out=ftmp[:, ti, :], in_=bank_ap(bg),
                                                                          func=AF.Silu)),
                             reads=[B_bank[bg]], writes=[B_tmp[ti]])
                        P.op("dve", (lambda e, bu=bu, ti=ti, j=j, s=s: e.tensor_tensor(
                            out=a_ap(j, s), in0=ftmp[:, ti, :], in1=bank_ap(bu), op=ALU.mult)),
                            reads=[B_tmp[ti], B_bank[bu]], writes=[B_a[j][s]])
            for m in range(8):
                sl = load_slot(f"DN{f}_{m}")
                for s in range(nsub):
                    cols = slice(s * 512, (s + 1) * 512)
                    b = nxt("bank", 8)
                    mm_group(bank_ap(b), [(ring[:, sl, j * 128:(j + 1) * 128], a_ap(j, s)) for j in range(NJ)],
                             [B_a[j][s] for j in range(NJ)] + [B_ring[sl]], b)
                    P.op("dve", (lambda e, b=b, m=m, cols=cols: e.scalar_tensor_tensor(
                        out=xs[:, m, cols], in0=bank_ap(b), scalar=0.5, in1=xs[:, m, cols],
                        op0=ALU.mult, op1=ALU.add)),
                        reads=[B_bank[b]], writes=[B_x[m][s]])
                    if m == 7 and after_last is not None:
                        after_last(s)

        def load_x(tok0, s):
            P.op("sp", dma_fn(xs[:, :, s * 512:(s + 1) * 512], xTv[:, :, tok0 + s * 512: tok0 + (s + 1) * 512]),
                 writes=[B_x[m][s] for m in range(8)], dma=f"xl{s}")

        def store_x(own0, s, from_fo):
            if from_fo:
                P.op("sp", dma_fn(outv[:, :, own0 + s * 512: own0 + (s + 1) * 512], fo_ap(s)),
                     reads=[B_a[j][s] for j in range(16)], dma=f"st{s}")
            else:
                P.op("sp", dma_fn(outv[:, :, own0 + s * 512: own0 + (s + 1) * 512], xs[:, :, s * 512:(s + 1) * 512]),
                     reads=[B_x[m][s] for m in range(8)], dma=f"st{s}")

        def w_in_phase(mt):
            nsub, blk0, full = mt["nsub"], mt["blk0"], mt["full"]
            for cq in ([0, 1, 2] if full else [0, 2]):
                sl = load_slot(f"IN{cq}")
                for c in range(4):
                    for s in range(nsub):
                        cols = slice(s * 512, (s + 1) * 512)
                        b = nxt("bank", 8)
                        mm_group(bank_ap(b), [(ring[:, sl, c * 1024 + kc * 128: c * 1024 + (kc + 1) * 128],
                                               hs[:, kc, cols]) for kc in range(8)],
                                 [B_h[kc][s] for kc in range(8)] + [B_ring[sl]], b)
                        if cq == 0:
                            P.op("act", (lambda e, b=b, c=c, s=s: e.activation(
                                out=up32[:, c, 16 + s * 512: 16 + (s + 1) * 512], in_=bank_ap(b), func=AF.Copy)),
                                reads=[B_bank[b]], writes=[B_up[c][s]])
                        elif cq == 1:
                            P.op("act", (lambda e, b=b, c=c, cols=cols: e.activation(
                                out=Qt[:, c, cols], in_=bank_ap(b), func=AF.Copy, scale=0.125)),
                                reads=[B_bank[b]], writes=[B_q[c][s]])
                        else:
                            gsub = (blk0 // 4 + s) % 3
                            P.op("dve", (lambda e, b=b, c=c, gsub=gsub: e.tensor_copy(
                                out=Kt[:, c, gsub * 512:(gsub + 1) * 512], in_=bank_ap(b))),
                                reads=[B_bank[b]], writes=[B_k[c][gsub]])
            sl = load_slot("INV")
            for s in range(nsub):
                for tb in range(4):
                    gb = blk0 + s * 4 + tb
                    vs = gb % KT_BLOCKS
                    b = nxt("bank", 8)
                    mm_group(bank_ap(b),
                             [(hs[:, kc, s * 512 + tb * 128: s * 512 + (tb + 1) * 128],
                               ring[:, sl, kc * 512:(kc + 1) * 512]) for kc in range(8)],
                             [B_h[kc][s] for kc in range(8)] + [B_ring[sl]], b)
                    vview = Vt[:, vs, :].rearrange("p (a c) -> p a c", c=192)
                    bview = bank_ap(b).rearrange("p (a e d) -> p a e d", a=4, e=2)
                    P.op("dve", (lambda e, vview=vview, bview=bview: e.tensor_copy(
                        out=vview[:, :, 0:64], in_=bview[:, :, 0, :])),
                        reads=[B_bank[b]], writes=[B_v[vs]])
                    P.op("dve", (lambda e, vview=vview, bview=bview: e.tensor_copy(
                        out=vview[:, :, 128:192], in_=bview[:, :, 1, :])),
                        reads=[B_bank[b]], writes=[B_v[vs]])
                    src = ones64 if full else hflagt
                    P.op("act", (lambda e, vview=vview, src=src: e.activation(
                        out=vview[:, :, 64:128], in_=src.unsqueeze(1).to_broadcast([128, 4, 64]), func=AF.Copy)),
                        reads=[B_const], writes=[B_v[vs]])

        def save_carry(ntok):
            P.op("dve", lambda e: e.tensor_copy(out=ucarry[:], in_=up32[:, :, ntok: ntok + 16]),
                 reads=[B_up[g][(ntok // 512) - 1] for g in range(4)], writes=[B_ucarry])

        def pooling(mt):
            ntok = mt["ntok"]
            L = 16 + ntok
            th = []
            scr_b = lambda k: [B_pscr[k]] + B_tmp[0:5]
            th.append(lambda: P.op("dve", lambda e: e.tensor_copy(out=up32[:, :, 0:16], in_=ucarry[:]),
                                   reads=[B_ucarry], writes=B_upre))
            for g, w in enumerate(WINDOWS):
                src_ap = lambda lo, hi, g=g: up32[:, g, lo:hi]
                src_b = [B_up[g][0], B_up[g][1], B_upre[g]]
                v0 = 0
                k = 0
                st = 1
                while st < w:
                    v = v0 + st
                    dst_ap = (lambda lo, hi, k=k: pscr[:, k, lo:hi])
                    th.append(lambda dst_ap=dst_ap, src_ap=src_ap, v=v, st=st, src_b=src_b, k=k: P.op(
                        "dve", (lambda e: e.tensor_tensor(out=dst_ap(v, L), in0=src_ap(v, L),
                                                          in1=src_ap(v - st, L - st), op=ALU.add)),
                        reads=src_b, writes=scr_b(k)))
                    src_ap, src_b = dst_ap, scr_b(k)
                    v0 = v
                    k ^= 1
                    st *= 2
                th.append(lambda src_ap=src_ap, g=g, w=w, src_b=src_b: P.op(
                    "dve", (lambda e: e.scalar_tensor_tensor(
                        out=MX[:, g, 0:ntok], in0=src_ap(16, L), scalar=1.0 / w, in1=up32[:, g, 16:L],
                        op0=ALU.mult, op1=ALU.subtract)),
                    reads=src_b + [B_up[g][0], B_up[g][1]], writes=[B_mx[g]]))
                if mt["first_own"]:
                    th.append(lambda src_ap=src_ap, g=g, src_b=src_b: P.op(
                        "dve", (lambda e: e.tensor_tensor(out=tmp16[:], in0=src_ap(16, 32), in1=invct[:, g, :],
                                                          op=ALU.mult)),
                        reads=src_b + [B_const], writes=[B_t16]))
                    th.append(lambda g=g: P.op(
                        "dve", (lambda e: e.tensor_tensor(out=MX[:, g, 0:16], in0=tmp16[:], in1=up32[:, g, 16:32],
                                                          op=ALU.subtract)),
                        reads=[B_t16, B_up[g][0]], writes=[B_mx[g]]))
            th.append(lambda: save_carry(ntok))
            return th

        def pool_w_phase(mt):
            sl = load_slot("PW")
            for g in range(4):
                for s in range(mt["nsub"]):
                    cols = slice(s * 512, (s + 1) * 512)
                    b = nxt("bank", 8)
                    mm_group(bank_ap(b), [(ring[:, sl, g * 128:(g + 1) * 128], MX[:, g, cols])],
                             [B_mx[g], B_ring[sl]], b)
                    P.op("act", (lambda e, b=b, g=g, cols=cols: e.activation(
                        out=YP[:, g, cols], in_=bank_ap(b), func=AF.Identity,
                        scale=cvt[:, C_PS + g:C_PS + g + 1])),
                        reads=[B_bank[b], B_const], writes=[B_yp[g][s]])

        def attention(mt, side=()):
            blk0 = mt["blk0"]
            q_lo, q_hi = blk0, blk0 + 7
            units = []
            for h in range(8):
                for j in range(blk0 - 4, blk0 + 8):
                    units.append((h, j, max(j, q_lo), min(j + 4, q_hi)))
            state = {}
            SB = [(0, 1), (2, 3)]
            OB = [(4, 5), (6, 7)]

            def issue_S(u):
                h, j, qa, qb = units[u]
                p, r0 = h // 2, 64 * (h % 2)
                nq = qb - qa + 1
                N = nq * 128
                si = nxt("spair", 2)
                b0, b1 = SB[si]
                col = (j % KT_BLOCKS) * 128
                ql = (qa - blk0) * 128
                ta, tb = qa - j, qb - j
                nA = max(0, min(tb, 1) - ta + 1) * 128
                nB = N - nA
                segs = []
                if nA:
                    segs.append((b0 * 512, 0, nA))
                if nB:
                    segs.append((b1 * 512, nA, nB))

                def fn(t):
                    ins = None
                    for (pc, qc, n) in segs:
                        ins = t.matmul(ps[:, pc: pc + n], Kt[r0:r0 + 64, p, col:col + 128],
                                       Qt[r0:r0 + 64, p, ql + qc: ql + qc + n], start=True, stop=True)
                    return ins
                wb = ([B_bank[b0]] if nA else []) + ([B_bank[b1]] if nB else [])
                P.op("pe", fn, reads=[B_k[p][(j // 4) % 3]] + [B_q[p][s_] for s_ in sorted({(qa - blk0) // 4, (qb - blk0) // 4})],
                     writes=wb)
                pi = nxt("P", NP)
                cbias = cvt[:, C_CB + h:C_CB + h + 1]
                if nB:
                    P.op("act", (lambda e: e.activation(out=Pt[:, pi, nA:N], in_=ps[:, b1 * 512: b1 * 512 + nB],
                                                        func=AF.Exp, bias=cbias)),
                         reads=[B_bank[b1], B_const], writes=[B_P[pi]])
                    if tb == 4:
                        P.op("dve", (lambda e: e.memset(Pt[0:64, pi, N - 64:N], 0.0)), writes=[B_P[pi]])
                if nA:
                    P.op("dve", (lambda e: e.tensor_tensor(out=ps[:, b0 * 512: b0 * 512 + nA],
                                                           in0=ps[:, b0 * 512: b0 * 512 + nA],
                                                           in1=biasT[:, h, ta * 128: ta * 128 + nA], op=ALU.add)),
                         reads=[B_bias], writes=[B_bank[b0]])
                    P.op("act", (lambda e: e.activation(out=Pt[:, pi, 0:nA], in_=ps[:, b0 * 512: b0 * 512 + nA],
                                                        func=AF.Exp)),
                         reads=[B_bank[b0]], writes=[B_Pb[pi]])
                state[u] = pi

            def issue_PV(u):
                h, j, qa, qb = units[u]
                pi = state.pop(u)
                p, par = h // 2, h % 2
                ob = OB[h % 2]
                vs = j % KT_BLOCKS
                lhsT = Vt[:, vs, p * 192 + 64 * par: p * 192 + 64 * par + 128]
                if j == blk0 - 4:
                    def fz(t):
                        ins = None
                        for b in ob:
                            ins = t.matmul(bank_ap(b), zeros_bf[:, 0:128], zeros_bf[:, 0:512], start=True, stop=False,
                                           skip_group_check=True)
                        return ins
                    P.op("pe", fz, reads=[B_zeros], writes=[B_bank[ob[0]], B_bank[ob[1]]])
                segs = []
                for half in range(2):
                    lo, hi = max(qa, blk0 + 4 * half), min(qb, blk0 + 4 * half + 3)
                    if lo <= hi:
                        segs.append((ob[half], (lo - blk0 - 4 * half) * 128, (lo - qa) * 128, (hi - lo + 1) * 128))

                def fn(t):
                    ins = None
                    for _ in range(NFILL):
                        t.matmul(bank_ap(ob[0]), zeros_bf[:, 0:128], zeros_bf[:, 0:512], start=False, stop=True,
                                 skip_group_check=True)
                    for (b, oc, pc, n) in segs:
                        ins = t.matmul(bank_ap(b, n, oc), lhsT, Pt[:, pi, pc: pc + n], start=False, stop=True,
                                       skip_group_check=True)
                    return ins
                P.op("pe", fn, reads=[B_P[pi], B_Pb[pi], B_v[vs], B_zeros], writes=[B_bank[b] for (b, _, _, _) in segs] + [B_bank[ob[0]]])
                if j == blk0 + 7:
                    o0, d0 = 64 * par, 64 * (1 - par)
                    oall = ps[:, ob[0] * 512: ob[0] * 512 + 1024]
                    P.op("act", (lambda e: e.activation(out=rtmp[o0:o0 + 64, :], in_=oall[d0:d0 + 64, :], func=AF.Ln)),
                         reads=[B_bank[ob[0]], B_bank[ob[1]]], writes=[B_rtmp[par]])
                    P.op("act", (lambda e: e.activation(out=rtmp[o0:o0 + 64, :], in_=rtmp[o0:o0 + 64, :], func=AF.Exp,
                                                        scale=-1.0)),
                         reads=[B_rtmp[par]], writes=[B_rtmp[par]])
                    P.op("dve", (lambda e: e.tensor_tensor(out=ATT[o0:o0 + 64, p, :], in0=oall[o0:o0 + 64, :],
                                                           in1=rtmp[o0:o0 + 64, :], op=ALU.mult)),
                         reads=[B_bank[ob[0]], B_bank[ob[1]], B_rtmp[par]],
                         writes=[B_attp[p][par]])

            LAG = 1
            NFILL = 0
            n = len(units)
            side = list(side)
            for u in range(n + LAG):
                if u < n:
                    issue_S(u)
                if u - LAG >= 0:
                    issue_PV(u - LAG)
                if side and u % 3 == 2:
                    side.pop(0)()
            while side:
                side.pop(0)()

        def gates_phase(mt):
            for m in range(8):
                sl = load_slot(f"GB{m}")
                for s in range(mt["nsub"]):
                    cols = slice(s * 512, (s + 1) * 512)
                    bA, bB, bP, bT = (nxt("bank", 8) for _ in range(4))
                    hreads = [B_h[kc][s] for kc in range(8)] + [B_ring[sl]]
                    mm_group(bank_ap(bA), [(ring[:, sl, kc * 128:(kc + 1) * 128], hs[:, kc, cols])
                                           for kc in range(8)], hreads, bA)
                    mm_group(bank_ap(bB), [(ring[:, sl, 1024 + kc * 128: 1024 + (kc + 1) * 128], hs[:, kc, cols])
                                           for kc in range(8)], hreads, bB)
                    mm_group(bank_ap(bP), [(ring[:, sl, 2048 + kc * 128: 2048 + (kc + 1) * 128], YP[:, kc, cols])
                                           for kc in range(4)], [B_yp[g][s] for g in range(4)] + [B_ring[sl]], bP)
                    mm_group(bank_ap(bT), [(ring[:, sl, 2560 + kc * 128: 2560 + (kc + 1) * 128], ATT[:, kc, cols])
                                           for kc in range(4)],
                             flat(B_attp) + [B_ring[sl]], bT)
                    tA, tB = nxt("tmp", NTMP), nxt("tmp", NTMP)
                    P.op("act", (lambda e, bA=bA, tA=tA, m=m: e.activation(
                        out=ftmp[:, tA, :], in_=bank_ap(bA), func=AF.Sigmoid, bias=cvt[:, C_BG + m:C_BG + m + 1])),
                        reads=[B_bank[bA], B_const], writes=[B_tmp[tA]])
                    P.op("act", (lambda e, bB=bB, tB=tB, m=m: e.activation(
                        out=ftmp[:, tB, :], in_=bank_ap(bB), func=AF.Sigmoid,
                        bias=cvt[:, C_BG + 8 + m:C_BG + 8 + m + 1])),
                        reads=[B_bank[bB], B_const], writes=[B_tmp[tB]])
                    v1, v2 = nxt("vw", 4), nxt("vw", 4)
                    P.op("dve", (lambda e, tA=tA, bP=bP, v1=v1: e.tensor_tensor(
                        out=vw[:, v1, :], in0=ftmp[:, tA, :], in1=bank_ap(bP), op=ALU.mult)),
                        reads=[B_tmp[tA], B_bank[bP]], writes=[B_vw[v1]])
                    P.op("dve", (lambda e, tB=tB, bT=bT, v2=v2: e.tensor_tensor(
                        out=vw[:, v2, :], in0=ftmp[:, tB, :], in1=bank_ap(bT), op=ALU.mult)),
                        reads=[B_tmp[tB], B_bank[bT]], writes=[B_vw[v2]])
                    P.op("dve", (lambda e, v1=v1, v2=v2, m=m, cols=cols: e.tensor_tensor(
                        out=MG[:, m, cols], in0=vw[:, v1, :], in1=vw[:, v2, :], op=ALU.add)),
                        reads=[B_vw[v1], B_vw[v2]], writes=[B_mg[m][s]])

        def w_out_phase(mt, after_last=None):
            for mq in range(2):
                sl = load_slot(f"WO{mq}")
                for mm in range(4):
                    m = 4 * mq + mm
                    for s in range(mt["nsub"]):
                        cols = slice(s * 512, (s + 1) * 512)
                        b = nxt("bank", 8)
                        mm_group(bank_ap(b), [(ring[:, sl, mm * 1024 + kc * 128: mm * 1024 + (kc + 1) * 128],
                                               MG[:, kc, cols]) for kc in range(8)],
                                 [B_mg[kc][s] for kc in range(8)] + [B_ring[sl]], b)
                        P.op("dve", (lambda e, b=b, m=m, cols=cols: e.tensor_tensor(
                            out=xs[:, m, cols], in0=bank_ap(b), in1=xs[:, m, cols], op=ALU.add)),
                            reads=[B_bank[b]], writes=[B_x[m][s]])
                        if m == 7 and after_last is not None:
                            after_last(s)

        MTS = [dict(name="H", tok0=0, ntok=512, nsub=1, full=False, blk0=0, first_own=False),
               dict(name="A", tok0=512, ntok=1024, nsub=2, full=True, blk0=4, first_own=True),
               dict(name="B", tok0=1536, ntok=1024, nsub=2, full=True, blk0=12, first_own=False)]

        load_x(MTS[0]["tok0"], 0)
        load_x(MTS[1]["tok0"], 1)
        for im, mt in enumerate(MTS):
            nsub, full = mt["nsub"], mt["full"]
            nxt_mt = MTS[im + 1] if im + 1 < len(MTS) else None
            for s in range(nsub):
                norm(s, C_G1)
            if stage >= 2:
                def after_ffn1(s, mt=mt, nxt_mt=nxt_mt):
                    norm(s, C_G2)
                    if not mt["full"]:
                        load_x(nxt_mt["tok0"], s)
                ffn(1, nsub, after_last=after_ffn1)
            else:
                ffn(1, nsub)
            if stage >= 2:
                handoff(flat(B_a), mixer_bufs)
                w_in_phase(mt)
                if not full:
                    save_carry(mt["ntok"])
                else:
                    attention(mt, side=pooling(mt))
                    pool_w_phase(mt)
                    handoff(flat(B_up) + B_upre + flat(B_q), flat(B_mg))
                    handoff(B_mx, B_vw)
                    gates_phase(mt)
                    if stage >= 3:
                        w_out_phase(mt, after_last=lambda s: norm(s, C_G3))
                    else:
                        w_out_phase(mt)
                handoff(mixer_bufs, flat(B_a))
            if full:
                own0 = mt["tok0"] - HALO
                if stage >= 4:
                    def after_ffn2(s, own0=own0, nxt_mt=nxt_mt):
                        norm(s, C_GF, final=True)
                        store_x(own0, s, True)
                        if nxt_mt is not None:
                            load_x(nxt_mt["tok0"], s)
                    ffn(2, nsub, after_last=after_ffn2)
                else:
                    if stage >= 3:
                        ffn(2, nsub)
                    for s in range(nsub):
                        store_x(own0, s, False)
                        if nxt_mt is not None:
                            load_x(nxt_mt["tok0"], s)

        sems = {}
        for key in P.cnt:
            sems[key] = es.enter_context(nc.semaphore(f"s_{key}"))
        for e in ("pe", "act", "dve"):
            if e not in sems:
                sems[e] = es.enter_context(nc.semaphore(f"s_{e}"))

        def emit(engname, eng, tail=None):
            for fn, waits, ev, is_dma in P.ops[engname]:
                for k, v in waits:
                    eng.wait_ge(sems[k], v)
                ins = fn(eng)
                ins.then_inc(sems[ev[0]], 16 if is_dma else 1)
            if tail is not None:
                tail(eng)

        with nc.Block() as block:
            @block.sync
            def _(sync):
                def tail(e):
                    for k in ("st0", "st1"):
                        e.wait_ge(sems[k], P.cnt[k])
                emit("sp", sync, tail)

            @block.gpsimd
            def _(g):
                emit("pool", g)

            @block.tensor
            def _(t):
                emit("pe", t)

            @block.scalar
            def _(s):
                emit("act", s)

            @block.vector
            def _(v):
                emit("dve", v)
    return nc


_CACHE = {}


def kernel(**inputs):
    stage = int(inputs.pop("_stage", 4))
    inp = {k: np.asarray(v, dtype=np.float32) for k, v in inputs.items()}
    x = inp["x"]
    wst = _build_wst(inp)
    cv, bt = _build_consts(inp)
    in_maps = []
    for c in range(NCORES):
        b, t0 = c // 4, (c % 4) * OWN
        xT = np.zeros((D, TLOC), np.float32)
        xT[:, HALO:] = x[b, t0:t0 + OWN].T
        if t0 > 0:
            xT[:, :HALO] = x[b, t0 - HALO:t0].T
        in_maps.append({"xT": xT, "wst": wst, "cv": cv, "bt": bt, "cc": _core_consts(t0)})
    if stage not in _CACHE:
        _CACHE[stage] = build_nc(stage)
    res = run_bass_kernel_spmd(_CACHE[stage], in_maps, core_ids=list(range(NCORES)))
    out = np.empty((2, SEQ, D), np.float32)
    for c in range(NCORES):
        b, t0 = c // 4, (c % 4) * OWN
        out[b, t0:t0 + OWN] = res.results[c]["outT"].T
    return out
```

```python
import contextlib
import numpy as np
import concourse.bass as bass
import concourse.mybir as mybir
from concourse.bass_utils import run_bass_kernel_spmd

F32 = mybir.dt.float32
BF16 = mybir.dt.bfloat16
AF = mybir.ActivationFunctionType
ALU = mybir.AluOpType

D = 1024
DFF = 2816
NJ = DFF // 128
SEQ = 8192
NCORES = 8
OWN = 2048
HALO = 512
TLOC = OWN + HALO
EPS = 1e-6
NEG = -30000.0
WINDOWS = (2, 4, 8, 16)

RING_SLOTS = 5
SLOT_ELEMS = 4096
KT_BLOCKS = 12
VROW = 768

SLOT_SIZES = {}
SLOT_ORDER = []


def _def_slots():
    for f in (1, 2):
        for jp in range(NJ // 2):
            SLOT_SIZES[f"GU{f}_{jp}"] = 4096
        for m in range(8):
            SLOT_SIZES[f"DN{f}_{m}"] = DFF
    for cq in range(3):
        SLOT_SIZES[f"IN{cq}"] = 4096
    SLOT_SIZES["INV"] = 4096
    SLOT_SIZES["PW"] = 512
    for m in range(8):
        SLOT_SIZES[f"GB{m}"] = 3072
    for mq in range(2):
        SLOT_SIZES[f"WO{mq}"] = 4096
    off = 0
    for k, v in SLOT_SIZES.items():
        SLOT_ORDER.append((k, off, v))
        off += v
    return off


WST_TOT = _def_slots()
SLOT_OFF = {k: (o, n) for k, o, n in SLOT_ORDER}

C_G1, C_G2, C_G3, C_GF, C_PS, C_BG, C_CB = 0, 8, 16, 24, 32, 36, 52
C_TOT = 64


def _colchunk(W, c0, ncol=128):
    K = W.shape[0]
    return W[:, c0:c0 + ncol].reshape(K // 128, 128, ncol).transpose(1, 0, 2).reshape(128, -1)


def _build_wst(inp):
    wst = np.empty((128, WST_TOT), np.float32)

    def put(name, arr):
        o, n = SLOT_OFF[name]
        assert arr.shape == (128, n), (name, arr.shape, n)
        wst[:, o:o + n] = arr

    for f, pre in ((1, "ffn1"), (2, "ffn2")):
        wg, wu, wd = inp[pre + "_w_gate"][0], inp[pre + "_w_up"][0], inp[pre + "_w_down"][0]
        for jp in range(NJ // 2):
            parts = []
            for jj in range(2):
                j = 2 * jp + jj
                parts += [_colchunk(wg, j * 128), _colchunk(wu, j * 128)]
            put(f"GU{f}_{jp}", np.concatenate(parts, axis=1))
        for m in range(8):
            put(f"DN{f}_{m}", _colchunk(wd, m * 128))
    w_in = inp["w_in"][0]
    for cq in range(3):
        put(f"IN{cq}", np.concatenate([_colchunk(w_in, (4 * cq + c) * 128) for c in range(4)], axis=1))
    put("INV", _colchunk(w_in, 1536, 512))
    put("PW", inp["pool_w"][0].transpose(1, 0, 2).reshape(128, 512))
    wgt, wbp, wba = inp["w_gate"][0], inp["w_branch_pool"][0], inp["w_branch_attn"][0]
    for m in range(8):
        put(f"GB{m}", np.concatenate([_colchunk(wgt, m * 128), _colchunk(wgt, (8 + m) * 128),
                                      _colchunk(wbp, m * 128), _colchunk(wba, m * 128)], axis=1))
    w_out = inp["w_out"][0]
    for mq in range(2):
        put(f"WO{mq}", np.concatenate([_colchunk(w_out, (4 * mq + c) * 128) for c in range(4)], axis=1))
    return wst


def _build_consts(inp):
    cv = np.zeros((128, C_TOT), np.float32)
    cv[:, C_G1:C_G1 + 8] = inp["ffn1_norm"][0].reshape(8, 128).T
    cv[:, C_G2:C_G2 + 8] = inp["mix_norm"][0].reshape(8, 128).T
    cv[:, C_G3:C_G3 + 8] = inp["ffn2_norm"][0].reshape(8, 128).T
    cv[:, C_GF:C_GF + 8] = inp["final_norm"].reshape(8, 128).T
    cv[:, C_PS:C_PS + 4] = inp["pool_scale"][0].reshape(4, 128).T
    cv[:, C_BG:C_BG + 16] = inp["b_gate"][0].reshape(16, 128).T
    rb = inp["rel_bias"][0]
    cv[:, C_CB:C_CB + 8] = np.broadcast_to(rb[:, 128][None, :], (128, 8))
    k = np.arange(128)[:, None]
    q = np.arange(128)[None, :]
    bt = np.empty((128, 8, 256), np.float32)
    for t in range(2):
        idx = np.clip(128 * t + q - k, -64, 64) + 64
        bt[:, :, t * 128:(t + 1) * 128] = rb[:, idx].transpose(1, 0, 2)
    return cv, bt.reshape(128, 2048)


def _core_consts(t0):
    msk = np.zeros((128, 256), np.float32)
    msk[64:128, 0:64] = NEG
    invc = np.empty((128, 4, 16), np.float32)
    tpos = t0 + np.arange(16)
    for g, w in enumerate(WINDOWS):
        invc[:, g, :] = (1.0 / np.minimum(tpos + 1, w))[None, :]
    hflag = np.full((128, 64), 1.0 if t0 > 0 else 0.0, np.float32)
    ones = np.ones((128, 64), np.float32)
    return np.concatenate([msk, invc.reshape(128, 64), hflag, ones], axis=1)


class Buf:
    __slots__ = ("name", "w", "r")

    def __init__(self, name):
        self.name, self.w, self.r = name, None, []


class Prog:
    ENGS = ("pe", "act", "dve", "pool", "sp")

    def __init__(self):
        self.ops = {e: [] for e in self.ENGS}
        self.cnt = {}
        self.known = {e: {} for e in self.ENGS}

    def _ev(self, key, amount):
        self.cnt[key] = self.cnt.get(key, 0) + amount
        return (key, self.cnt[key])

    def op(self, eng, fn, reads=(), writes=(), dma=None):
        deps = {}

        def add(ev):
            if ev is not None and deps.get(ev[0], 0) < ev[1]:
                deps[ev[0]] = ev[1]

        for b in reads:
            add(b.w)
        for b in writes:
            add(b.w)
            for e in b.r:
                add(e)
        kn = self.known[eng]
        waits = []
        for k, v in deps.items():
            if k == "pe" and eng == "pe":
                continue
            if kn.get(k, 0) >= v:
                continue
            kn[k] = v
            waits.append((k, v))
        ev = self._ev(dma, 16) if dma is not None else self._ev(eng, 1)
        self.ops[eng].append((fn, waits, ev, dma is not None))
        for b in reads:
            b.r.append(ev)
        for b in writes:
            b.w = ev
            b.r = []
        return ev


def handoff(src, dst):
    for d in dst:
        for s in src:
            d.r.extend(s.r)
            if s.w is not None:
                d.r.append(s.w)


def build_nc(stage=4):
    nc = bass.Bass("TRN2", target_bir_lowering=False)
    xT = nc.dram_tensor("xT", [D, TLOC], F32, kind="ExternalInput").ap()
    wst = nc.dram_tensor("wst", [128, WST_TOT], F32, kind="ExternalInput").ap()
    cvd = nc.dram_tensor("cv", [128, C_TOT], F32, kind="ExternalInput").ap()
    btd = nc.dram_tensor("bt", [128, 2048], F32, kind="ExternalInput").ap()
    ccd = nc.dram_tensor("cc", [128, 448], F32, kind="ExternalInput").ap()
    outT = nc.dram_tensor("outT", [D, OWN], F32, kind="ExternalOutput").ap()
    xTv = xT.rearrange("(c p) t -> p c t", p=128)
    outv = outT.rearrange("(c p) t -> p c t", p=128)

    P = Prog()
    es = contextlib.ExitStack()
    with es:
        def sb(name, shape, dt):
            return es.enter_context(nc.sbuf_tensor(name, shape, dt))

        xs = sb("xs", [128, 8, 1024], F32)
        hs = sb("hs", [128, 8, 1024], BF16)
        AREG = 28800
        areg = sb("areg", [128, AREG], BF16)
        ring = sb("ring", [128, RING_SLOTS, SLOT_ELEMS], BF16)
        Kt = sb("Kt", [128, 4, KT_BLOCKS * 128], BF16)
        Vt = sb("Vt", [128, KT_BLOCKS, VROW], BF16)
        biasT = sb("biasT", [128, 8, 256], F32)
        cvt = sb("cvt", [128, C_TOT], F32)
        epst = sb("epst", [128, 2], F32)
        cct = sb("cct", [128, 448], F32)
        zeros_bf = sb("zeros_bf", [128, 128], BF16)
        rtmp = sb("rtmp", [128, 1024], F32)
        ones_bf = sb("ones_bf", [128, 128], BF16)
        ucarry = sb("ucarry", [128, 4, 16], F32)
        tmp16 = sb("tmp16", [128, 16], F32)
        NTMP = 6
        ftmp = sb("ftmp", [128, NTMP, 512], F32)
        NSQ = 3
        sqt = sb("sqt", [128, NSQ, 512], BF16)
        NP = 3
        Pt = sb("Pt", [128, NP, 640], BF16)
        ps = es.enter_context(nc.psum_tensor("ps", [128, 4096], F32))

        pscr = ftmp[:].rearrange("p a b -> p (a b)")[:, 0:2080].rearrange("p (k t) -> p k t", k=2)
        mskt = cct[:, 0:256]
        invct = cct[:, 256:320].rearrange("p (g t) -> p g t", g=4)
        hflagt = cct[:, 320:384]
        ones64 = cct[:, 384:448]

        ASUB = NJ * 512

        def a_ap(j, s):
            return areg[:, s * ASUB + j * 512: s * ASUB + (j + 1) * 512]

        def fo_ap(s):
            return areg[:, s * ASUB: s * ASUB + 8192].bitcast(F32).rearrange("p (c t) -> p c t", c=8)
        UPW = 1040
        up32 = areg[:, 0:8320].bitcast(F32).rearrange("p (g t) -> p g t", g=4)
        Qm = areg[:, 8320:16512].rearrange("p (c e t) -> p c e t", c=4, e=2)
        MX = areg[:, 16512:20608].rearrange("p (c t) -> p c t", c=4)
        YP = areg[:, 20608:24704].rearrange("p (c t) -> p c t", c=4)
        ATT = areg[:, 24704:28800].rearrange("p (c t) -> p c t", c=4)
        MG = areg[:, 0:8192].rearrange("p (c t) -> p c t", c=8)
        vw = areg[:, 16512:16512 + 4096].bitcast(F32).rearrange("p (k t) -> p k t", k=4)

        B_x = [[Buf(f"x{m}{s}") for s in range(2)] for m in range(8)]
        B_h = [[Buf(f"h{k}{s}") for s in range(2)] for k in range(8)]
        B_a = [[Buf(f"a{j}{s}") for s in range(2)] for j in range(NJ)]
        B_ring = [Buf(f"ring{i}") for i in range(RING_SLOTS)]
        B_bank = [Buf(f"bank{i}") for i in range(8)]
        B_tmp = [Buf(f"tmp{i}") for i in range(NTMP)]
        B_sq = [Buf(f"sq{i}") for i in range(NSQ)]
        B_P = [Buf(f"P{i}") for i in range(NP)]
        B_Pb = [Buf(f"Pb{i}") for i in range(NP)]
        B_k = [[Buf(f"k{c}{g}") for g in range(3)] for c in range(4)]
        B_v = [Buf(f"v{i}") for i in range(KT_BLOCKS)]
        B_up = [[Buf(f"up{g}{s}") for s in range(2)] for g in range(4)]
        B_upre = [Buf(f"upre{g}") for g in range(4)]
        B_q = [[Buf(f"q{c}{s}") for s in range(2)] for c in range(4)]
        B_mx = [Buf(f"mx{g}") for g in range(4)]
        B_yp = [[Buf(f"yp{g}{s}") for s in range(2)] for g in range(4)]
        B_att = [[Buf(f"att{qi}{par}") for par in range(2)] for qi in range(8)]
        B_pscr = [Buf("pscr0"), Buf("pscr1")]
        B_attp = [[Buf(f"attp{p}{par}") for par in range(2)] for p in range(4)]
        B_mg = [[Buf(f"mg{m}{s}") for s in range(2)] for m in range(8)]
        B_vw = [Buf(f"vw{i}") for i in range(4)]
        B_ucarry = Buf("ucarry")
        B_t16 = Buf("t16")
        B_const = Buf("const")
        B_bias = Buf("bias")
        B_g32 = Buf("g32")
        B_ones = Buf("ones")
        B_zeros = Buf("zeros")
        B_qz = [Buf("qz0"), Buf("qz1")]
        B_rtmp = [Buf("rtmp0"), Buf("rtmp1")]

        def flat(ll):
            return [b for l in ll for b in l]

        mixer_bufs = (B_qz + flat(B_up) + B_upre + flat(B_q) + B_mx + flat(B_yp) + flat(B_att) + flat(B_attp)
                      + B_pscr + flat(B_mg) + B_vw)

        rot = {"bank": 0, "tmp": 0, "sq": 0, "P": 0, "ring": 0, "spair": 0, "vw": 0}

        def nxt(kind, n):
            i = rot[kind]
            rot[kind] = (i + 1) % n
            return i

        def bank_ap(b, n=512, off=0):
            return ps[:, b * 512 + off: b * 512 + off + n]

        def dma_fn(out, in_):
            return lambda e: e.dma_start(out=out, in_=in_)

        P.op("sp", dma_fn(cvt[:], cvd), writes=[B_const], dma="cst")
        P.op("sp", dma_fn(biasT[:].rearrange("p h q -> p (h q)"), btd), writes=[B_bias], dma="cst")
        P.op("sp", dma_fn(cct[:], ccd), writes=[Buf("cc")], dma="cst")
        full_cst = ("cst", P.cnt["cst"])
        B_const.w = full_cst
        B_bias.w = full_cst

        P.op("dve", lambda e: e.memset(epst[:], EPS), writes=[B_g32])
        P.op("dve", lambda e: e.memset(ones_bf[:], 1.0), writes=[B_ones])
        P.op("dve", lambda e: e.memset(zeros_bf[:], 0.0), writes=[B_zeros])
        P.op("dve", lambda e: e.tensor_tensor(out=biasT[:], in0=biasT[:],
                                              in1=mskt.unsqueeze(1).to_broadcast([128, 8, 256]), op=ALU.add),
             reads=[B_const], writes=[B_bias])

        nslot = [0]

        def load_slot(name):
            o, n = SLOT_OFF[name]
            i = nxt("ring", RING_SLOTS)
            nslot[0] += 1
            extra = [B_x[0][0]] if 2 <= nslot[0] <= RING_SLOTS else []
            P.op("pool", dma_fn(ring[:, i, 0:n], wst[:, o:o + n]), reads=extra, writes=[B_ring[i]], dma=f"ring{i}")
            return i

        def mm_group(out_ap, pairs, reads, bank):
            n = len(pairs)

            def fn(t):
                ins = None
                for i, (l, r) in enumerate(pairs):
                    ins = t.matmul(out_ap, l, r, start=(i == 0), stop=(i == n - 1))
                return ins
            P.op("pe", fn, reads=reads, writes=[B_bank[bank]])

        def norm(s, gcol, final=False):
            cols = slice(s * 512, (s + 1) * 512)
            b = nxt("bank", 8)
            for kc in range(8):
                qi = nxt("sq", NSQ)
                P.op("act", (lambda e, kc=kc, qi=qi: e.activation(out=sqt[:, qi, :], in_=xs[:, kc, cols],
                                                                  func=AF.Square)),
                     reads=[B_x[kc][s]], writes=[B_sq[qi]])

                def fn(t, kc=kc, qi=qi):
                    return t.matmul(bank_ap(b), ones_bf[:], sqt[:, qi, :], start=(kc == 0), stop=(kc == 7))
                P.op("pe", fn, reads=[B_sq[qi], B_ones], writes=[B_bank[b]])
            ti = nxt("tmp", NTMP)
            P.op("act", lambda e: e.activation(out=ftmp[:, ti, :], in_=bank_ap(b), func=AF.Ln,
                                               scale=1.0 / D, bias=epst[:, 0:1]),
                 reads=[B_bank[b], B_g32], writes=[B_tmp[ti]])
            P.op("act", lambda e: e.activation(out=ftmp[:, ti, :], in_=ftmp[:, ti, :], func=AF.Exp, scale=-0.5),
                 reads=[B_tmp[ti]], writes=[B_tmp[ti]])
            for kc in range(8):
                if final:
                    P.op("dve", (lambda e, kc=kc: e.scalar_tensor_tensor(
                        out=fo_ap(s)[:, kc, :], in0=xs[:, kc, cols], scalar=cvt[:, gcol + kc:gcol + kc + 1],
                        in1=ftmp[:, ti, :], op0=ALU.mult, op1=ALU.mult)),
                        reads=[B_x[kc][s], B_tmp[ti], B_const], writes=[B_a[2 * kc][s], B_a[2 * kc + 1][s]])
                else:
                    P.op("dve", (lambda e, kc=kc: e.scalar_tensor_tensor(
                        out=hs[:, kc, cols], in0=xs[:, kc, cols], scalar=cvt[:, gcol + kc:gcol + kc + 1],
                        in1=ftmp[:, ti, :], op0=ALU.mult, op1=ALU.mult)),
                        reads=[B_x[kc][s], B_tmp[ti], B_const], writes=[B_h[kc][s]])

        def ffn(f, nsub, after_last=None):
            for jp in range(NJ // 2):
                sl = load_slot(f"GU{f}_{jp}")
                for jj in range(2):
                    j = 2 * jp + jj
                    for s in range(nsub):
                        cols = slice(s * 512, (s + 1) * 512)
                        bg = nxt("bank", 8)
                        bu = nxt("bank", 8)
                        hreads = [B_h[kc][s] for kc in range(8)] + [B_ring[sl]]
                        for which, bb in ((0, bg), (1, bu)):
                            base = (jj * 2 + which) * 1024
                            mm_group(bank_ap(bb),
                                     [(ring[:, sl, base + kc * 128: base + (kc + 1) * 128], hs[:, kc, cols])
                                      for kc in range(8)], hreads, bb)
                        ti = nxt("tmp", NTMP)
                        P.op("act", (lambda e, bg=bg, ti=ti: e.activation(out=ftmp[:, ti, :], in_=bank_ap(bg),
                                                                          func=AF.Silu)),
                             reads=[B_bank[bg]], writes=[B_tmp[ti]])
                        P.op("dve", (lambda e, bu=bu, ti=ti, j=j, s=s: e.tensor_tensor(
                            out=a_ap(j, s), in0=ftmp[:, ti, :], in1=bank_ap(bu), op=ALU.mult)),
                            reads=[B_tmp[ti], B_bank[bu]], writes=[B_a[j][s]])
            for m in range(8):
                sl = load_slot(f"DN{f}_{m}")
                for s in range(nsub):
                    cols = slice(s * 512, (s + 1) * 512)
                    b = nxt("bank", 8)
                    mm_group(bank_ap(b), [(ring[:, sl, j * 128:(j + 1) * 128], a_ap(j, s)) for j in range(NJ)],
                             [B_a[j][s] for j in range(NJ)] + [B_ring[sl]], b)
                    P.op("dve", (lambda e, b=b, m=m, cols=cols: e.scalar_tensor_tensor(
                        out=xs[:, m, cols], in0=bank_ap(b), scalar=0.5, in1=xs[:, m, cols],
                        op0=ALU.mult, op1=ALU.add)),
                        reads=[B_bank[b]], writes=[B_x[m][s]])
                    if m == 7 and after_last is not None:
                        after_last(s)

        def load_x(tok0, s):
            P.op("sp", dma_fn(xs[:, :, s * 512:(s + 1) * 512], xTv[:, :, tok0 + s * 512: tok0 + (s + 1) * 512]),
                 writes=[B_x[m][s] for m in range(8)], dma=f"xl{s}")

        def store_x(own0, s, from_fo):
            if from_fo:
                P.op("sp", dma_fn(outv[:, :, own0 + s * 512: own0 + (s + 1) * 512], fo_ap(s)),
                     reads=[B_a[j][s] for j in range(16)], dma=f"st{s}")
            else:
                P.op("sp", dma_fn(outv[:, :, own0 + s * 512: own0 + (s + 1) * 512], xs[:, :, s * 512:(s + 1) * 512]),
                     reads=[B_x[m][s] for m in range(8)], dma=f"st{s}")

        def w_in_phase(mt):
            nsub, blk0, full = mt["nsub"], mt["blk0"], mt["full"]
            if full:
                P.op("dve", lambda e: e.memset(Qm[64:128, :, 0, :], 0.0), writes=[B_qz[0]])
                P.op("dve", lambda e: e.memset(Qm[0:64, :, 1, :], 0.0), writes=[B_qz[1]])
            for cq in ([0, 1, 2] if full else [0, 2]):
                sl = load_slot(f"IN{cq}")
                for c in range(4):
                    for s in range(nsub):
                        cols = slice(s * 512, (s + 1) * 512)
                        b = nxt("bank", 8)
                        mm_group(bank_ap(b), [(ring[:, sl, c * 1024 + kc * 128: c * 1024 + (kc + 1) * 128],
                                               hs[:, kc, cols]) for kc in range(8)],
                                 [B_h[kc][s] for kc in range(8)] + [B_ring[sl]], b)
                        if cq == 0:
                            P.op("act", (lambda e, b=b, c=c, s=s: e.activation(
                                out=up32[:, c, 16 + s * 512: 16 + (s + 1) * 512], in_=bank_ap(b), func=AF.Copy)),
                                reads=[B_bank[b]], writes=[B_up[c][s]])
                        elif cq == 1:
                            for par in range(2):
                                P.op("act", (lambda e, b=b, c=c, cols=cols, par=par: e.activation(
                                    out=Qm[64 * par:64 * par + 64, c, par, cols],
                                    in_=ps[64 * par:64 * par + 64, b * 512:(b + 1) * 512], func=AF.Copy, scale=0.125)),
                                    reads=[B_bank[b]], writes=[B_q[c][s]])
                        else:
                            gsub = (blk0 // 4 + s) % 3
                            P.op("dve", (lambda e, b=b, c=c, gsub=gsub: e.tensor_copy(
                                out=Kt[:, c, gsub * 512:(gsub + 1) * 512], in_=bank_ap(b))),
                                reads=[B_bank[b]], writes=[B_k[c][gsub]])
            sl = load_slot("INV")
            for s in range(nsub):
                for tb in range(4):
                    gb = blk0 + s * 4 + tb
                    vs = gb % KT_BLOCKS
                    b = nxt("bank", 8)
                    mm_group(bank_ap(b),
                             [(hs[:, kc, s * 512 + tb * 128: s * 512 + (tb + 1) * 128],
                               ring[:, sl, kc * 512:(kc + 1) * 512]) for kc in range(8)],
                             [B_h[kc][s] for kc in range(8)] + [B_ring[sl]], b)
                    vview = Vt[:, vs, :].rearrange("p (a c) -> p a c", c=192)
                    bview = bank_ap(b).rearrange("p (a e d) -> p a e d", a=4, e=2)
                    P.op("dve", (lambda e, vview=vview, bview=bview: e.tensor_copy(
                        out=vview[:, :, 0:64], in_=bview[:, :, 0, :])),
                        reads=[B_bank[b]], writes=[B_v[vs]])
                    P.op("dve", (lambda e, vview=vview, bview=bview: e.tensor_copy(
                        out=vview[:, :, 128:192], in_=bview[:, :, 1, :])),
                        reads=[B_bank[b]], writes=[B_v[vs]])
                    src = ones64 if full else hflagt
                    P.op("act", (lambda e, vview=vview, src=src: e.activation(
                        out=vview[:, :, 64:128], in_=src.unsqueeze(1).to_broadcast([128, 4, 64]), func=AF.Copy)),
                        reads=[B_const], writes=[B_v[vs]])

        def save_carry(ntok):
            P.op("dve", lambda e: e.tensor_copy(out=ucarry[:], in_=up32[:, :, ntok: ntok + 16]),
                 reads=[B_up[g][(ntok // 512) - 1] for g in range(4)], writes=[B_ucarry])

        def pooling(mt):
            ntok = mt["ntok"]
            L = 16 + ntok
            th = []
            scr_b = lambda k: [B_pscr[k]] + B_tmp[0:5]
            th.append(lambda: P.op("dve", lambda e: e.tensor_copy(out=up32[:, :, 0:16], in_=ucarry[:]),
                                   reads=[B_ucarry], writes=B_upre))
            for g, w in enumerate(WINDOWS):
                src_ap = lambda lo, hi, g=g: up32[:, g, lo:hi]
                src_b = [B_up[g][0], B_up[g][1], B_upre[g]]
                v0 = 0
                k = 0
                st = 1
                while st < w:
                    v = v0 + st
                    dst_ap = (lambda lo, hi, k=k: pscr[:, k, lo:hi])
                    th.append(lambda dst_ap=dst_ap, src_ap=src_ap, v=v, st=st, src_b=src_b, k=k: P.op(
                        "dve", (lambda e: e.tensor_tensor(out=dst_ap(v, L), in0=src_ap(v, L),
                                                          in1=src_ap(v - st, L - st), op=ALU.add)),
                        reads=src_b, writes=scr_b(k)))
                    src_ap, src_b = dst_ap, scr_b(k)
                    v0 = v
                    k ^= 1
                    st *= 2
                th.append(lambda src_ap=src_ap, g=g, w=w, src_b=src_b: P.op(
                    "dve", (lambda e: e.scalar_tensor_tensor(
                        out=MX[:, g, 0:ntok], in0=src_ap(16, L), scalar=1.0 / w, in1=up32[:, g, 16:L],
                        op0=ALU.mult, op1=ALU.subtract)),
                    reads=src_b + [B_up[g][0], B_up[g][1]], writes=[B_mx[g]]))
                if mt["first_own"]:
                    th.append(lambda src_ap=src_ap, g=g, src_b=src_b: P.op(
                        "dve", (lambda e: e.tensor_tensor(out=tmp16[:], in0=src_ap(16, 32), in1=invct[:, g, :],
                                                          op=ALU.mult)),
                        reads=src_b + [B_const], writes=[B_t16]))
                    th.append(lambda g=g: P.op(
                        "dve", (lambda e: e.tensor_tensor(out=MX[:, g, 0:16], in0=tmp16[:], in1=up32[:, g, 16:32],
                                                          op=ALU.subtract)),
                        reads=[B_t16, B_up[g][0]], writes=[B_mx[g]]))
            th.append(lambda: save_carry(ntok))
            return th

        def pool_w_phase(mt):
            sl = load_slot("PW")
            for g in range(4):
                for s in range(mt["nsub"]):
                    cols = slice(s * 512, (s + 1) * 512)
                    b = nxt("bank", 8)
                    mm_group(bank_ap(b), [(ring[:, sl, g * 128:(g + 1) * 128], MX[:, g, cols])],
                             [B_mx[g], B_ring[sl]], b)
                    P.op("act", (lambda e, b=b, g=g, cols=cols: e.activation(
                        out=YP[:, g, cols], in_=bank_ap(b), func=AF.Identity,
                        scale=cvt[:, C_PS + g:C_PS + g + 1])),
                        reads=[B_bank[b], B_const], writes=[B_yp[g][s]])

        def attention(mt, side=()):
            blk0 = mt["blk0"]
            q_lo, q_hi = blk0, blk0 + 7
            units = []
            for h in range(8):
                for j in range(blk0 - 4, blk0 + 8):
                    units.append((h, j, max(j, q_lo), min(j + 4, q_hi)))
            state = {}
            SB = [(0, 1), (2, 3)]
            OB = [(4, 5), (6, 7)]

            def issue_S(u):
                h, j, qa, qb = units[u]
                p, r0 = h // 2, 64 * (h % 2)
                nq = qb - qa + 1
                N = nq * 128
                si = nxt("spair", 2)
                b0, b1 = SB[si]
                col = (j % KT_BLOCKS) * 128
                ql = (qa - blk0) * 128
                ta, tb = qa - j, qb - j
                nA = max(0, min(tb, 1) - ta + 1) * 128
                nB = N - nA
                segs = []
                if nA:
                    segs.append((b0 * 512, 0, nA))
                if nB:
                    segs.append((b1 * 512, nA, nB))

                def fn(t):
                    ins = None
                    for (pc, qc, n) in segs:
                        ins = t.matmul(ps[:, pc: pc + n], Kt[:, p, col:col + 128],
                                       Qm[:, p, h % 2, ql + qc: ql + qc + n], start=True, stop=True)
                    return ins
                wb = ([B_bank[b0]] if nA else []) + ([B_bank[b1]] if nB else [])
                P.op("pe", fn, reads=[B_k[p][(j // 4) % 3], B_qz[h % 2]] + [B_q[p][s_] for s_ in sorted({(qa - blk0) // 4, (qb - blk0) // 4})],
                     writes=wb)
                pi = nxt("P", NP)
                cbias = cvt[:, C_CB + h:C_CB + h + 1]
                if nB:
                    P.op("act", (lambda e: e.activation(out=Pt[:, pi, nA:N], in_=ps[:, b1 * 512: b1 * 512 + nB],
                                                        func=AF.Exp, bias=cbias)),
                         reads=[B_bank[b1], B_const], writes=[B_P[pi]])
                    if tb == 4:
                        P.op("dve", (lambda e: e.memset(Pt[0:64, pi, N - 64:N], 0.0)), writes=[B_P[pi]])
                if nA:
                    P.op("dve", (lambda e: e.tensor_tensor(out=ps[:, b0 * 512: b0 * 512 + nA],
                                                           in0=ps[:, b0 * 512: b0 * 512 + nA],
                                                           in1=biasT[:, h, ta * 128: ta * 128 + nA], op=ALU.add)),
                         reads=[B_bias], writes=[B_bank[b0]])
                    P.op("act", (lambda e: e.activation(out=Pt[:, pi, 0:nA], in_=ps[:, b0 * 512: b0 * 512 + nA],
                                                        func=AF.Exp)),
                         reads=[B_bank[b0]], writes=[B_Pb[pi]])
                state[u] = pi

            def issue_PV(u):
                h, j, qa, qb = units[u]
                pi = state.pop(u)
                p, par = h // 2, h % 2
                ob = OB[h % 2]
                vs = j % KT_BLOCKS
                lhsT = Vt[:, vs, p * 192 + 64 * par: p * 192 + 64 * par + 128]
                if j == blk0 - 4:
                    def fz(t):
                        ins = None
                        for b in ob:
                            ins = t.matmul(bank_ap(b), zeros_bf[:, 0:128], hs[:, 0, 0:512], start=True, stop=False,
                                           skip_group_check=True)
                        return ins
                    P.op("pe", fz, reads=[B_zeros, B_h[0][0]], writes=[B_bank[ob[0]], B_bank[ob[1]]])
                segs = []
                for half in range(2):
                    lo, hi = max(qa, blk0 + 4 * half), min(qb, blk0 + 4 * half + 3)
                    if lo <= hi:
                        segs.append((ob[half], (lo - blk0 - 4 * half) * 128, (lo - qa) * 128, (hi - lo + 1) * 128))

                def fn(t):
                    ins = None
                    for _ in range(NFILL):
                        t.matmul(bank_ap(ob[0]), zeros_bf[:, 0:128], hs[:, 0, 0:512], start=False, stop=True,
                                 skip_group_check=True)
                    for (b, oc, pc, n) in segs:
                        ins = t.matmul(bank_ap(b, n, oc), lhsT, Pt[:, pi, pc: pc + n], start=False, stop=True,
                                       skip_group_check=True)
                    return ins
                P.op("pe", fn, reads=[B_P[pi], B_Pb[pi], B_v[vs], B_zeros], writes=[B_bank[b] for (b, _, _, _) in segs] + [B_bank[ob[0]]])
                if j == blk0 + 7:
                    o0, d0 = 64 * par, 64 * (1 - par)
                    oall = ps[:, ob[0] * 512: ob[0] * 512 + 1024]
                    P.op("act", (lambda e: e.activation(out=rtmp[o0:o0 + 64, :], in_=oall[d0:d0 + 64, :], func=AF.Ln)),
                         reads=[B_bank[ob[0]], B_bank[ob[1]]], writes=[B_rtmp[par]])
                    P.op("act", (lambda e: e.activation(out=rtmp[o0:o0 + 64, :], in_=rtmp[o0:o0 + 64, :], func=AF.Exp,
                                                        scale=-1.0)),
                         reads=[B_rtmp[par]], writes=[B_rtmp[par]])
                    P.op("dve", (lambda e: e.tensor_tensor(out=ATT[o0:o0 + 64, p, :], in0=oall[o0:o0 + 64, :],
                                                           in1=rtmp[o0:o0 + 64, :], op=ALU.mult)),
                         reads=[B_bank[ob[0]], B_bank[ob[1]], B_rtmp[par]],
                         writes=[B_attp[p][par]])

            LAG = 1
            NFILL = 0
            n = len(units)
            side = list(side)
            for u in range(n + LAG):
                if u < n:
                    issue_S(u)
                if u - LAG >= 0:
                    issue_PV(u - LAG)
                if side and u % 3 == 2:
                    side.pop(0)()
            while side:
                side.pop(0)()

        def gates_phase(mt):
            for m in range(8):
                sl = load_slot(f"GB{m}")
                for s in range(mt["nsub"]):
                    cols = slice(s * 512, (s + 1) * 512)
                    bA, bB, bP, bT = (nxt("bank", 8) for _ in range(4))
                    hreads = [B_h[kc][s] for kc in range(8)] + [B_ring[sl]]
                    mm_group(bank_ap(bA), [(ring[:, sl, kc * 128:(kc + 1) * 128], hs[:, kc, cols])
                                           for kc in range(8)], hreads, bA)
                    mm_group(bank_ap(bB), [(ring[:, sl, 1024 + kc * 128: 1024 + (kc + 1) * 128], hs[:, kc, cols])
                                           for kc in range(8)], hreads, bB)
                    mm_group(bank_ap(bP), [(ring[:, sl, 2048 + kc * 128: 2048 + (kc + 1) * 128], YP[:, kc, cols])
                                           for kc in range(4)], [B_yp[g][s] for g in range(4)] + [B_ring[sl]], bP)
                    mm_group(bank_ap(bT), [(ring[:, sl, 2560 + kc * 128: 2560 + (kc + 1) * 128], ATT[:, kc, cols])
                                           for kc in range(4)],
                             flat(B_attp) + [B_ring[sl]], bT)
                    tA, tB = nxt("tmp", NTMP), nxt("tmp", NTMP)
                    P.op("act", (lambda e, bA=bA, tA=tA, m=m: e.activation(
                        out=ftmp[:, tA, :], in_=bank_ap(bA), func=AF.Sigmoid, bias=cvt[:, C_BG + m:C_BG + m + 1])),
                        reads=[B_bank[bA], B_const], writes=[B_tmp[tA]])
                    P.op("act", (lambda e, bB=bB, tB=tB, m=m: e.activation(
                        out=ftmp[:, tB, :], in_=bank_ap(bB), func=AF.Sigmoid,
                        bias=cvt[:, C_BG + 8 + m:C_BG + 8 + m + 1])),
                        reads=[B_bank[bB], B_const], writes=[B_tmp[tB]])
                    v1, v2 = nxt("vw", 4), nxt("vw", 4)
                    P.op("dve", (lambda e, tA=tA, bP=bP, v1=v1: e.tensor_tensor(
                        out=vw[:, v1, :], in0=ftmp[:, tA, :], in1=bank_ap(bP), op=ALU.mult)),
                        reads=[B_tmp[tA], B_bank[bP]], writes=[B_vw[v1]])
                    P.op("dve", (lambda e, tB=tB, bT=bT, v2=v2: e.tensor_tensor(
                        out=vw[:, v2, :], in0=ftmp[:, tB, :], in1=bank_ap(bT), op=ALU.mult)),
                        reads=[B_tmp[tB], B_bank[bT]], writes=[B_vw[v2]])
                    P.op("dve", (lambda e, v1=v1, v2=v2, m=m, cols=cols: e.tensor_tensor(
                        out=MG[:, m, cols], in0=vw[:, v1, :], in1=vw[:, v2, :], op=ALU.add)),
                        reads=[B_vw[v1], B_vw[v2]], writes=[B_mg[m][s]])

        def w_out_phase(mt, after_last=None):
            for mq in range(2):
                sl = load_slot(f"WO{mq}")
                for mm in range(4):
                    m = 4 * mq + mm
                    for s in range(mt["nsub"]):
                        cols = slice(s * 512, (s + 1) * 512)
                        b = nxt("bank", 8)
                        mm_group(bank_ap(b), [(ring[:, sl, mm * 1024 + kc * 128: mm * 1024 + (kc + 1) * 128],
                                               MG[:, kc, cols]) for kc in range(8)],
                                 [B_mg[kc][s] for kc in range(8)] + [B_ring[sl]], b)
                        P.op("dve", (lambda e, b=b, m=m, cols=cols: e.tensor_tensor(
                            out=xs[:, m, cols], in0=bank_ap(b), in1=xs[:, m, cols], op=ALU.add)),
                            reads=[B_bank[b]], writes=[B_x[m][s]])
                        if m == 7 and after_last is not None:
                            after_last(s)

        MTS = [dict(name="H", tok0=0, ntok=512, nsub=1, full=False, blk0=0, first_own=False),
               dict(name="A", tok0=512, ntok=1024, nsub=2, full=True, blk0=4, first_own=True),
               dict(name="B", tok0=1536, ntok=1024, nsub=2, full=True, blk0=12, first_own=False)]

        load_x(MTS[0]["tok0"], 0)
        load_x(MTS[1]["tok0"], 1)
        for im, mt in enumerate(MTS):
            nsub, full = mt["nsub"], mt["full"]
            nxt_mt = MTS[im + 1] if im + 1 < len(MTS) else None
            for s in range(nsub):
                norm(s, C_G1)
            if stage >= 2:
                def after_ffn1(s, mt=mt, nxt_mt=nxt_mt):
                    norm(s, C_G2)
                    if not mt["full"]:
                        load_x(nxt_mt["tok0"], s)
                ffn(1, nsub, after_last=after_ffn1)
            else:
                ffn(1, nsub)
            if stage >= 2:
                handoff(flat(B_a), mixer_bufs)
                w_in_phase(mt)
                if not full:
                    save_carry(mt["ntok"])
                else:
                    attention(mt, side=pooling(mt))
                    pool_w_phase(mt)
                    handoff(flat(B_up) + B_upre + flat(B_q) + B_qz, flat(B_mg))
                    handoff(B_mx, B_vw)
                    gates_phase(mt)
                    if stage >= 3:
                        w_out_phase(mt, after_last=lambda s: norm(s, C_G3))
                    else:
                        w_out_phase(mt)
                handoff(mixer_bufs, flat(B_a))
            if full:
                own0 = mt["tok0"] - HALO
                if stage >= 4:
                    def after_ffn2(s, own0=own0, nxt_mt=nxt_mt):
                        norm(s, C_GF, final=True)
                        store_x(own0, s, True)
                        if nxt_mt is not None:
                            load_x(nxt_mt["tok0"], s)
                    ffn(2, nsub, after_last=after_ffn2)
                else:
                    if stage >= 3:
                        ffn(2, nsub)
                    for s in range(nsub):
                        store_x(own0, s, False)
                        if nxt_mt is not None:
                            load_x(nxt_mt["tok0"], s)

        sems = {}
        for key in P.cnt:
            sems[key] = es.enter_context(nc.semaphore(f"s_{key}"))
        for e in ("pe", "act", "dve"):
            if e not in sems:
                sems[e] = es.enter_context(nc.semaphore(f"s_{e}"))

        def emit(engname, eng, tail=None):
            for fn, waits, ev, is_dma in P.ops[engname]:
                for k, v in waits:
                    eng.wait_ge(sems[k], v)
                ins = fn(eng)
                ins.then_inc(sems[ev[0]], 16 if is_dma else 1)
            if tail is not None:
                tail(eng)

        with nc.Block() as block:
            @block.sync
            def _(sync):
                def tail(e):
                    for k in ("st0", "st1"):
                        e.wait_ge(sems[k], P.cnt[k])
                emit("sp", sync, tail)

            @block.gpsimd
            def _(g):
                emit("pool", g)

            @block.tensor
            def _(t):
                emit("pe", t)

            @block.scalar
            def _(s):
                emit("act", s)

            @block.vector
            def _(v):
                emit("dve", v)
    return nc


_CACHE = {}


def kernel(**inputs):
    stage = int(inputs.pop("_stage", 4))
    inp = {k: np.asarray(v, dtype=np.float32) for k, v in inputs.items()}
    x = inp["x"]
    wst = _build_wst(inp)
    cv, bt = _build_consts(inp)
    in_maps = []
    for c in range(NCORES):
        b, t0 = c // 4, (c % 4) * OWN
        xT = np.zeros((D, TLOC), np.float32)
        xT[:, HALO:] = x[b, t0:t0 + OWN].T
        if t0 > 0:
            xT[:, :HALO] = x[b, t0 - HALO:t0].T
        in_maps.append({"xT": xT, "wst": wst, "cv": cv, "bt": bt, "cc": _core_consts(t0)})
    if stage not in _CACHE:
        _CACHE[stage] = build_nc(stage)
    res = run_bass_kernel_spmd(_CACHE[stage], in_maps, core_ids=list(range(NCORES)))
    out = np.empty((2, SEQ, D), np.float32)
    for c in range(NCORES):
        b, t0 = c // 4, (c % 4) * OWN
        out[b, t0:t0 + OWN] = res.results[c]["outT"].T
    return out
```
